# Optimizing a Trainium2 kernel written in Bass

```python
import math
import jax, jax.numpy as jnp
from jax import lax
import numpy as np


D_MODEL = 1024
BATCH = 8
SEQ = 8192
DEPTH = 4

N_MIXERS = 2
HEAD_DIM = 64
A_Q_HEADS = D_MODEL // HEAD_DIM
A_KV_HEADS = A_Q_HEADS // 4
A_GROUP = A_Q_HEADS // A_KV_HEADS
WINDOW = 128
B_HEADS = D_MODEL // HEAD_DIM
BLOCK = 128
REL_BUCKETS = 32
REL_MAX_DIST = 128
D_FF = 2816
EPS = 1e-6
FORGET_BIAS_INIT = 2.0
N_A_LAYERS = (DEPTH + 1) // 2
N_B_LAYERS = DEPTH // 2
A_IN = (A_Q_HEADS + 2 * A_KV_HEADS) * HEAD_DIM
B_IN = 3 * B_HEADS * HEAD_DIM + B_HEADS

kernel_name = 'hybrid_swa_sink_fox_macaron_adaln'


def rmsnorm(x, g):
    xf = x.astype(jnp.float32)
    y = xf * lax.rsqrt(jnp.mean(xf * xf, axis=-1, keepdims=True) + EPS) * g.astype(jnp.float32)
    return y.astype(x.dtype)


def modulate(x, g, shift, scale):
    return rmsnorm(x, g) * (1 + scale[:, None, :]) + shift[:, None, :]


def swiglu(h, w13, w2):
    gate, up = jnp.split(h @ w13, 2, axis=-1)
    return (jax.nn.silu(gate) * up) @ w2


def rel_bucket(dist):
    n = jnp.maximum(dist, 0)
    max_exact = REL_BUCKETS // 2
    nf = jnp.maximum(n, 1).astype(jnp.float32)
    large = max_exact + (jnp.log(nf / max_exact) / math.log(REL_MAX_DIST / max_exact)
                         * (REL_BUCKETS - max_exact)).astype(jnp.int32)
    large = jnp.minimum(large, REL_BUCKETS - 1)
    return jnp.where(n < max_exact, n, large)


def swa_mixer(h, w_in, w_out, q_g, k_g, sink, rel_bias):
    B, S, _ = h.shape
    nblk = S // BLOCK
    proj = h @ w_in
    q, k, v = jnp.split(proj, [A_Q_HEADS * HEAD_DIM, (A_Q_HEADS + A_KV_HEADS) * HEAD_DIM], axis=-1)
    q = rmsnorm(q.reshape(B, S, A_KV_HEADS, A_GROUP, HEAD_DIM), q_g) * (HEAD_DIM ** -0.5)
    k = rmsnorm(k.reshape(B, S, A_KV_HEADS, HEAD_DIM), k_g)
    v = v.reshape(B, S, A_KV_HEADS, HEAD_DIM)
    pad = ((0, 0), (BLOCK, 0), (0, 0), (0, 0))
    kp = jnp.pad(k, pad)
    vp = jnp.pad(v, pad)
    qi = jnp.arange(BLOCK)[:, None] + BLOCK
    kj = jnp.arange(2 * BLOCK)[None, :]
    dist = qi - kj
    band = (dist >= 0) & (dist < WINDOW)
    bias = jnp.transpose(rel_bias[rel_bucket(dist)], (2, 0, 1)).astype(jnp.float32)
    bias = bias.reshape(A_KV_HEADS, A_GROUP, BLOCK, 2 * BLOCK)
    sink_f = sink.astype(jnp.float32).reshape(A_KV_HEADS, A_GROUP)[None, :, :, None]

    def block(i):
        start = i * BLOCK
        qb = lax.dynamic_slice_in_dim(q, start, BLOCK, axis=1)
        kb = lax.dynamic_slice_in_dim(kp, start, 2 * BLOCK, axis=1)
        vb = lax.dynamic_slice_in_dim(vp, start, 2 * BLOCK, axis=1)
        s = jnp.einsum('bqhgd,bkhd->bhgqk', qb, kb).astype(jnp.float32) + bias
        valid = band & (start - BLOCK + kj >= 0)
        s = jnp.where(valid, s, -jnp.inf)
        m = jnp.maximum(jnp.max(s, axis=-1), sink_f)
        p = jnp.exp(s - m[..., None])
        denom = jnp.sum(p, axis=-1) + jnp.exp(sink_f - m)
        p = p / denom[..., None]
        return jnp.einsum('bhgqk,bkhd->bqhgd', p.astype(vb.dtype), vb)

    out = lax.map(block, jnp.arange(nblk))
    out = jnp.moveaxis(out, 0, 1).reshape(B, S, A_Q_HEADS * HEAD_DIM)
    return out @ w_out


def fox_mixer(h, w_in, w_out, b_f, q_g, k_g):
    B, S, _ = h.shape
    nblk = S // BLOCK
    HD = B_HEADS * HEAD_DIM
    proj = h @ w_in
    q, k, v, fl = jnp.split(proj, [HD, 2 * HD, 3 * HD], axis=-1)
    q = rmsnorm(q.reshape(B, S, B_HEADS, HEAD_DIM), q_g) * (HEAD_DIM ** -0.5)
    k = rmsnorm(k.reshape(B, S, B_HEADS, HEAD_DIM), k_g)
    v = v.reshape(B, S, B_HEADS, HEAD_DIM)
    log_f = jax.nn.log_sigmoid(fl.astype(jnp.float32) + b_f.astype(jnp.float32))
    F = jnp.transpose(jnp.cumsum(log_f, axis=1), (0, 2, 1))
    kpos = jnp.arange(S)

    def block(i):
        start = i * BLOCK
        qb = lax.dynamic_slice_in_dim(q, start, BLOCK, axis=1)
        Fq = lax.dynamic_slice_in_dim(F, start, BLOCK, axis=2)
        s = jnp.einsum('bqhd,bkhd->bhqk', qb, k).astype(jnp.float32)
        s = s + Fq[..., None] - F[:, :, None, :]
        qpos = start + jnp.arange(BLOCK)
        s = jnp.where(kpos[None, :] <= qpos[:, None], s, -jnp.inf)
        p = jax.nn.softmax(s, axis=-1)
        return jnp.einsum('bhqk,bkhd->bqhd', p.astype(v.dtype), v)

    out = lax.map(block, jnp.arange(nblk))
    out = jnp.moveaxis(out, 0, 1).reshape(B, S, HD)
    return out @ w_out


def setup_inputs(seed: int = 0) -> dict:
    key = jax.random.key(seed)
    ks = jax.random.split(key, 20)
    D = D_MODEL

    def nrm(k, shape, s):
        return jax.random.normal(k, shape, jnp.float32) * s

    return {
        'x': nrm(ks[0], (BATCH, SEQ, D), 1.0),
        'c': nrm(ks[1], (BATCH, D), 1.0),
        'ada_w': nrm(ks[2], (DEPTH, D, 9 * D), 0.5 * D ** -0.5),
        'ada_b': nrm(ks[3], (DEPTH, 9 * D), 0.02),
        'norm_g': 1.0 + nrm(ks[4], (DEPTH, 3, D), 0.02),
        'ffn_w13': nrm(ks[5], (DEPTH, 2, D, 2 * D_FF), D ** -0.5),
        'ffn_w2': nrm(ks[6], (DEPTH, 2, D_FF, D), D_FF ** -0.5),
        'rel_bias': nrm(ks[7], (REL_BUCKETS, A_Q_HEADS), 0.5),
        'swa_w_in': nrm(ks[8], (N_A_LAYERS, D, A_IN), D ** -0.5),
        'swa_w_out': nrm(ks[9], (N_A_LAYERS, A_Q_HEADS * HEAD_DIM, D), (A_Q_HEADS * HEAD_DIM) ** -0.5),
        'swa_q_g': 1.0 + nrm(ks[10], (N_A_LAYERS, HEAD_DIM), 0.02),
        'swa_k_g': 1.0 + nrm(ks[11], (N_A_LAYERS, HEAD_DIM), 0.02),
        'swa_sink': nrm(ks[12], (N_A_LAYERS, A_Q_HEADS), 0.5),
        'fox_w_in': nrm(ks[13], (N_B_LAYERS, D, B_IN), D ** -0.5),
        'fox_w_out': nrm(ks[14], (N_B_LAYERS, B_HEADS * HEAD_DIM, D), (B_HEADS * HEAD_DIM) ** -0.5),
        'fox_b_f': FORGET_BIAS_INIT + nrm(ks[15], (N_B_LAYERS, B_HEADS), 0.1),
        'fox_q_g': 1.0 + nrm(ks[16], (N_B_LAYERS, HEAD_DIM), 0.02),
        'fox_k_g': 1.0 + nrm(ks[17], (N_B_LAYERS, HEAD_DIM), 0.02),
    }


def reference(x, c, ada_w, ada_b, norm_g, ffn_w13, ffn_w2, rel_bias,
              swa_w_in, swa_w_out, swa_q_g, swa_k_g, swa_sink,
              fox_w_in, fox_w_out, fox_b_f, fox_q_g, fox_k_g):
    B = x.shape[0]
    c_act = jax.nn.silu(c)
    for layer in range(DEPTH):
        mod = (c_act @ ada_w[layer] + ada_b[layer]).reshape(B, 3, 3, D_MODEL)
        h = modulate(x, norm_g[layer, 0], mod[:, 0, 0], mod[:, 0, 1])
        x = x + 0.5 * mod[:, 0, 2][:, None, :] * swiglu(h, ffn_w13[layer, 0], ffn_w2[layer, 0])
        h = modulate(x, norm_g[layer, 1], mod[:, 1, 0], mod[:, 1, 1])
        j = layer // N_MIXERS
        if layer % N_MIXERS == 0:
            y = swa_mixer(h, swa_w_in[j], swa_w_out[j], swa_q_g[j], swa_k_g[j], swa_sink[j], rel_bias)
        else:
            y = fox_mixer(h, fox_w_in[j], fox_w_out[j], fox_b_f[j], fox_q_g[j], fox_k_g[j])
        x = x + mod[:, 1, 2][:, None, :] * y
        h = modulate(x, norm_g[layer, 2], mod[:, 2, 0], mod[:, 2, 1])
        x = x + 0.5 * mod[:, 2, 2][:, None, :] * swiglu(h, ffn_w13[layer, 1], ffn_w2[layer, 1])
    return x
```

```python
import math
import os
from contextlib import ExitStack

import numpy as np
import concourse.bass as bass
import concourse.mybir as mybir
from concourse.bass_utils import run_bass_kernel_spmd

F32 = mybir.dt.float32
BF16 = mybir.dt.bfloat16
AF = mybir.ActivationFunctionType
ALU = mybir.AluOpType

D = 1024
DFF = 2816
HD = 64
NH = 16
KC = D // 128
FC = DFF // 128
EPS = 1e-6
DEPTH = 4
NEG = -30000.0


class Eng:
    def __init__(self, name, h, sem):
        self.name, self.h, self.sem = name, h, sem
        self.count = 0
        self.waited = {}

    def wait(self, sem, val):
        k = id(sem)
        if self.waited.get(k, 0) >= val:
            return
        self.h.wait_ge(sem, val)
        self.waited[k] = val


class DS:
    def __init__(self, sem):
        self.sem = sem
        self.count = 0


class Buf:
    __slots__ = ("name", "w", "r", "ds")

    def __init__(self, name, ds=None):
        self.name = name
        self.w = {}
        self.r = {}
        self.ds = ds


class K:
    def __init__(self, nc, es, n_dsem=48):
        self.nc = nc
        mk = lambda n: es.enter_context(nc.semaphore(n))
        self.pe = Eng("pe", nc.tensor, mk("s_pe"))
        self.act = Eng("act", nc.scalar, mk("s_act"))
        self.dve = Eng("dve", nc.vector, mk("s_dve"))
        self.pool = Eng("pool", nc.gpsimd, mk("s_pool"))
        self.sp = Eng("sp", nc.sync, mk("s_sp"))
        self.bar_sem = mk("s_bar")
        self.bar_count = 0
        self.dpool = [DS(mk(f"s_d{i}")) for i in range(n_dsem)]
        self.dnext = 0
        self.engs = [self.pe, self.act, self.dve, self.pool]

    def buf(self, name, dma=False):
        ds = None
        if dma:
            ds = self.dpool[self.dnext]
            self.dnext += 1
        return Buf(name, ds)

    def release_dsems(self):
        self.dnext = 0

    def _deps(self, eng, reads, writes, skip_ds=None):
        for b in reads:
            for (sem, val, src) in b.w.values():
                if skip_ds is not None and sem is skip_ds.sem:
                    continue
                eng.wait(sem, val)
        for b in writes:
            for (sem, val, src) in list(b.w.values()) + list(b.r.values()):
                if src is eng:
                    continue
                if skip_ds is not None and sem is skip_ds.sem:
                    continue
                eng.wait(sem, val)

    def op(self, eng, fn, reads=(), writes=()):
        self._deps(eng, reads, writes)
        ins = fn(eng.h)
        eng.count += 1
        ins.then_inc(eng.sem, 1)
        tok = (eng.sem, eng.count, eng)
        k = id(eng.sem)
        for b in reads:
            b.r[k] = tok
        for b in writes:
            b.w = {k: tok}
            b.r = {}
        return ins

    def dma(self, q, out, in_, slot, reads=(), writes=(), **kw):
        ds = slot.ds
        self._deps(q, reads, writes, skip_ds=ds)
        ins = q.h.dma_start(out=out, in_=in_, **kw)
        ds.count += 16
        ins.then_inc(ds.sem, 16)
        tok = (ds.sem, ds.count, None)
        k = id(ds.sem)
        for b in reads:
            b.r[k] = tok
        for b in writes:
            if k in b.w:
                b.w[k] = tok
            else:
                b.w = {k: tok}
                b.r = {}
        return ins

    def barrier(self):
        sp = self.sp
        for e in self.engs:
            if e.count:
                sp.wait(e.sem, e.count)
        for ds in self.dpool:
            if ds.count:
                sp.wait(ds.sem, ds.count)
        self.bar_count += 1
        sp.h.sem_inc(self.bar_sem, 1)
        for e in self.engs:
            e.h.wait_ge(self.bar_sem, self.bar_count)


class Ctx:
    pass


_uid = [0]


def sb(nc, es, name, shape, dt):
    _uid[0] += 1
    return es.enter_context(nc.sbuf_tensor(f"{name}_{_uid[0]}", shape, dt))


def bcast_row(dram_ap_1d, n):
    return dram_ap_1d.partition_broadcast(128)


def load_bcast(k, C, es, name, src_1d, n=D):
    nc = k.nc
    t = sb(nc, es, name, [128, n], F32)
    b = k.buf(name, dma=True)
    k.dma(k.sp, t[:], bcast_row(src_1d, n), b, writes=[b])
    return t, b


def emit_norm_stats(k, C, xt, xb, ss, ssb, col, junk, junkb):
    k.op(k.act, lambda e: e.activation(out=junk[:], in_=xt, func=AF.Square, accum_out=ss[:, col:col + 1]),
         reads=[xb], writes=[ssb, junkb])


def phase_mods(k, C, layers):
    nc = k.nc
    with ExitStack() as es:
        c_sb = sb(nc, es, "c_sb", [128, KC], F32)
        cact = sb(nc, es, "cact", [128, KC], F32)
        cb = k.buf("c", dma=True)
        cab = k.buf("cact")
        k.dma(k.sp, c_sb[:], C.c.rearrange("(kc p) -> p kc", p=128), cb, writes=[cb],
              allow_slow_non_contiguous=True)
        k.op(k.act, lambda e: e.activation(out=cact[:], in_=c_sb[:], func=AF.Silu), reads=[cb], writes=[cab])
        NR = 3
        wt = [sb(nc, es, f"adaw{i}", [128, KC, 512], F32) for i in range(NR)]
        wb = [k.buf(f"adaw{i}", dma=True) for i in range(NR)]
        bt = [sb(nc, es, f"adab{i}", [1, 512], F32) for i in range(NR)]
        bb = [k.buf(f"adab{i}", dma=True) for i in range(NR)]
        ot = [sb(nc, es, f"modo{i}", [1, 512], F32) for i in range(NR)]
        ob = [k.buf(f"modo{i}", dma=True) for i in range(NR)]
        n = 0
        for l in layers:
            for ct in range(9 * D // 512):
                s = n % NR
                pb = n % 2
                k.dma(k.sp, wt[s][:], C.ada_w[l, :, ct * 512:(ct + 1) * 512].rearrange("(kc p) n -> p kc n", p=128),
                      wb[s], writes=[wb[s]])
                k.dma(k.sp, bt[s][:], C.ada_b[l:l + 1, ct * 512:(ct + 1) * 512], bb[s], writes=[bb[s]])

                def mm(e, s=s, pb=pb):
                    for kc in range(KC):
                        ins = e.matmul(C.ps[pb][0:1, :], lhsT=cact[:, kc:kc + 1], rhs=wt[s][:, kc, :],
                                       start=(kc == 0), stop=(kc == KC - 1))
                    return ins
                k.op(k.pe, mm, reads=[cab, wb[s]], writes=[C.psb[pb]])
                k.op(k.dve, lambda e, s=s, pb=pb: e.tensor_tensor(out=ot[s][:], in0=C.ps[pb][0:1, :], in1=bt[s][:], op=ALU.add),
                     reads=[C.psb[pb], bb[s]], writes=[ob[s]])
                k.dma(k.sp, C.modv[l:l + 1, ct * 512:(ct + 1) * 512], ot[s][:], ob[s], reads=[ob[s]])
                n += 1
        k.barrier()
    k.release_dsems()


def mod_prologue(k, C, es, l, i, gate_scale, ga=None):
    nc = k.nc
    a_t, a_b = load_bcast(k, C, es, "a_b", C.modv[l, (i * 3 + 1) * D:(i * 3 + 2) * D])
    sh_t, sh_b = load_bcast(k, C, es, "sh_b", C.norm_g[l, i, :])
    if ga is None:
        ga_t, ga_b = load_bcast(k, C, es, "ga_b", C.modv[l, (i * 3 + 2) * D:(i * 3 + 3) * D])
    else:
        ga_t, ga_b = ga
        k.dma(k.sp, ga_t[:], bcast_row(C.modv[l, (i * 3 + 2) * D:(i * 3 + 3) * D], D), ga_b, writes=[ga_b])
    k.op(k.dve, lambda e: e.scalar_tensor_tensor(out=a_t[:], in0=a_t[:], scalar=1.0, in1=sh_t[:], op0=ALU.add, op1=ALU.mult),
         reads=[a_b, sh_b], writes=[a_b])
    k.op(k.dve, lambda e: e.tensor_scalar(out=ga_t[:], in0=ga_t[:], scalar1=float(gate_scale), scalar2=None, op0=ALU.mult),
         reads=[ga_b], writes=[ga_b])
    k.dma(k.sp, sh_t[:], bcast_row(C.modv[l, (i * 3 + 0) * D:(i * 3 + 1) * D], D), sh_b, writes=[sh_b])
    return (a_t, a_b), (sh_t, sh_b), (ga_t, ga_b)


def load_scaled_weight(k, C, stage, stage_b, cnt, dst_ap, dst_buf, src_ap, ga_t, ga_b, rows=128, p0=0):
    s = cnt % len(stage)
    k.dma(k.sp, stage[s][p0:p0 + rows, :], src_ap, stage_b[s], writes=[stage_b[s]])
    k.op(k.dve, lambda e: e.tensor_tensor(out=dst_ap, in0=stage[s][p0:p0 + rows, :], in1=ga_t[p0:p0 + rows, :], op=ALU.mult),
         reads=[stage_b[s], ga_b], writes=[dst_buf])


def emit_transposes(k, C, h_t, h_b, hT, hT_b, s, tpi):
    tpb = C.ps[tpi][:].bitcast(BF16)

    def tr(e):
        for kc in range(KC):
            ins = e.transpose(tpb[:, kc * 128:(kc + 1) * 128], h_t[:, kc * 128:(kc + 1) * 128], C.ident[:])
        return ins
    k.op(k.pe, tr, reads=[h_b, C.constb], writes=[C.psb[tpi]])
    k.op(k.act, lambda e: e.copy(out=hT[:, :, s * 128:(s + 1) * 128], in_=tpb.rearrange("p (k t) -> p k t", k=KC)),
         reads=[C.psb[tpi]], writes=[hT_b])


def phase_ffn(k, C, l, i, src, dst):
    nc = k.nc
    S = C.S
    NT = S // 512
    fi = i // 2
    with ExitStack() as es:
        xin = [sb(nc, es, f"xin{j}", [128, D], F32) for j in range(4)]
        xinb = [k.buf(f"xin{j}", dma=True) for j in range(4)]
        xres = [sb(nc, es, f"xres{j}", [128, D], F32) for j in range(2)]
        xresb = [k.buf(f"xres{j}", dma=True) for j in range(2)]
        (a_t, a_b), (sh_t, sh_b), (ga_t, ga_b) = mod_prologue(k, C, es, l, i, 0.5, ga=(xin[0], xinb[0]))
        w13 = sb(nc, es, "w13", [128, KC, 2 * DFF], BF16)
        w2 = sb(nc, es, "w2", [128, FC, D], BF16)
        w13b = k.buf("w13", dma=True)
        w2b = [k.buf(f"w2_{j}") for j in range(FC)]
        for kc in range(KC):
            k.dma(k.pool, w13[:, kc, :], C.ffn_w13[l, fi, kc * 128:(kc + 1) * 128, :], w13b, writes=[w13b])
        stage = xres
        stage_b = xresb
        for j in range(FC):
            load_scaled_weight(k, C, stage, stage_b, j, w2[:, j, :], w2b[j], C.ffn_w2[l, fi, j * 128:(j + 1) * 128, :], ga_t, ga_b)

        hbf = [sb(nc, es, f"hbf{j}", [128, D], BF16) for j in range(4)]
        hbfb = [k.buf(f"hbf{j}") for j in range(4)]
        hT = sb(nc, es, "hT", [128, KC, 512], BF16)
        hTb = [k.buf(f"hT{j}") for j in range(4)]
        actT = sb(nc, es, "actT", [128, FC, 512], BF16)
        actb = [k.buf(f"act{j}") for j in range(FC)]
        sg = [sb(nc, es, f"sg{j}", [128, 512], F32) for j in range(2)]
        sgb = [k.buf(f"sg{j}") for j in range(2)]
        ss = [sb(nc, es, f"ss{j}", [128, 4], F32) for j in range(2)]
        ssb = [k.buf(f"ss{j}") for j in range(2)]
        rs = [sb(nc, es, f"rs{j}", [128, 4], F32) for j in range(2)]
        rsb = [k.buf(f"rs{j}") for j in range(2)]

        def L(t):
            for s in range(4):
                r0 = t * 512 + s * 128
                k.dma(k.sp, xin[s][:], src[r0:r0 + 128, :], xinb[s], writes=[xinb[s]])

        def N(t):
            p = t % 2
            k.op(k.dve, lambda e: e.memset(ss[p][:], 0.0), writes=[ssb[p]])
            for s in range(4):
                emit_norm_stats(k, C, xin[s][:], xinb[s], ss[p], ssb[p], s, hbf[s], hbfb[s])
            k.op(k.act, lambda e: e.activation(out=rs[p][:], in_=ss[p][:], func=AF.Sqrt, bias=C.eps_t[:, 0:1], scale=1.0 / D),
                 reads=[ssb[p], C.constb], writes=[rsb[p]])
            k.op(k.dve, lambda e: e.reciprocal(out=rs[p][:], in_=rs[p][:]), reads=[rsb[p]], writes=[rsb[p]])
            for s in range(4):
                k.op(k.dve, lambda e, s=s: e.scalar_tensor_tensor(out=xin[s][:], in0=xin[s][:], scalar=rs[p][:, s:s + 1], in1=a_t[:],
                                                                 op0=ALU.mult, op1=ALU.mult),
                     reads=[xinb[s], rsb[p], a_b], writes=[xinb[s]])
                k.op(k.dve, lambda e, s=s: e.tensor_tensor(out=hbf[s][:], in0=xin[s][:], in1=sh_t[:], op=ALU.add),
                     reads=[xinb[s], sh_b], writes=[hbfb[s]])

        def T(t):
            for s in range(4):
                emit_transposes(k, C, hbf[s], hbfb[s], hT, hTb[s], s, 6 + (s % 2))

        def U(t, mid=None):
            for j in range(FC):
                pb = j % 2
                for half, bank in ((0, pb), (1, 2 + pb)):
                    col = half * DFF + j * 128

                    def mm(e, col=col, bank=bank):
                        for kc in range(KC):
                            ins = e.matmul(C.ps[bank][:, :], lhsT=w13[:, kc, col:col + 128], rhs=hT[:, kc, :],
                                           start=(kc == 0), stop=(kc == KC - 1))
                        return ins
                    k.op(k.pe, mm, reads=[w13b] + hTb, writes=[C.psb[bank]])
                k.op(k.act, lambda e, pb=pb: e.activation(out=sg[pb][:], in_=C.ps[pb][:, :], func=AF.Silu),
                     reads=[C.psb[pb]], writes=[sgb[pb]])
                k.op(k.dve, lambda e, pb=pb, j=j: e.tensor_tensor(out=actT[:, j, :], in0=sg[pb][:], in1=C.ps[2 + pb][:, :], op=ALU.mult),
                     reads=[sgb[pb], C.psb[2 + pb]], writes=[actb[j]])
                if mid is not None and j == 5:
                    mid()

        def Dn(t):
            for s in range(4):
                r0 = t * 512 + s * 128
                xs = s % 2
                k.dma(k.sp, xres[xs][:], src[r0:r0 + 128, :], xresb[xs], writes=[xresb[xs]])
                for n in range(2):
                    bank = 4 + (2 * s + n) % 2

                    def mm(e, s=s, n=n, bank=bank):
                        for j in range(FC):
                            ins = e.matmul(C.ps[bank][:, :], lhsT=actT[:, j, s * 128:(s + 1) * 128], rhs=w2[:, j, n * 512:(n + 1) * 512],
                                           start=(j == 0), stop=(j == FC - 1))
                        return ins
                    k.op(k.pe, mm, reads=actb + w2b, writes=[C.psb[bank]])
                    k.op(k.dve, lambda e, xs=xs, n=n, bank=bank: e.tensor_tensor(out=xres[xs][:, n * 512:(n + 1) * 512],
                                                                                  in0=xres[xs][:, n * 512:(n + 1) * 512],
                                                                                  in1=C.ps[bank][:, :], op=ALU.add),
                         reads=[xresb[xs], C.psb[bank]], writes=[xresb[xs]])
                k.dma(k.sp, dst[r0:r0 + 128, :], xres[xs][:], xresb[xs], reads=[xresb[xs]])

        L(0)
        N(0)
        T(0)
        for t in range(NT):
            nxt = None
            if t + 1 < NT:
                L(t + 1)
                nxt = (lambda t=t: N(t + 1))
            U(t, nxt)
            if t + 1 < NT:
                T(t + 1)
            Dn(t)
        k.barrier()
    k.release_dsems()


def build(S, plan, n_layers=DEPTH):
    nc = bass.Bass("TRN2", target_bir_lowering=False)
    C = Ctx()
    C.S = S
    inp = lambda name, shape, dt=F32: nc.dram_tensor(name, shape, dt, kind="ExternalInput").ap()
    C.x = inp("x", [S, D])
    C.c = inp("c", [D])
    C.ada_w = inp("ada_w", [DEPTH, D, 9 * D])
    C.ada_b = inp("ada_b", [DEPTH, 9 * D])
    C.norm_g = inp("norm_g", [DEPTH, 3, D])
    C.ffn_w13 = inp("ffn_w13", [DEPTH, 2, D, 2 * DFF])
    C.ffn_w2 = inp("ffn_w2", [DEPTH, 2, DFF, D])
    C.swa_w_in = inp("swa_w_in", [2, D, 1536])
    C.swa_w_out = inp("swa_w_out", [2, D, D])
    C.swa_q_g = inp("swa_q_g", [2, HD])
    C.swa_k_g = inp("swa_k_g", [2, HD])
    C.swa_sink = inp("swa_sink", [2, NH])
    C.swa_bias = inp("swa_bias", [128, 4096])
    C.fox_w_in = inp("fox_w_in", [2, D, 3088])
    C.fox_w_out = inp("fox_w_out", [2, D, D])
    C.fox_b_f = inp("fox_b_f", [2, NH])
    C.fox_q_g = inp("fox_q_g", [2, HD])
    C.fox_k_g = inp("fox_k_g", [2, HD])
    C.k_ident = inp("k_ident", [128, 128])
    C.k_swamask = inp("k_swamask", [128, 4096])
    C.k_negtri = inp("k_negtri", [128, 128])
    C.out = nc.dram_tensor("out", [S, D], F32, kind="ExternalOutput").ap()
    C.modv = nc.dram_tensor("modv", [DEPTH, 9 * D], F32, kind="Internal").ap()
    C.qx = nc.dram_tensor("qx", [NH, 65, S], BF16, kind="Internal").ap()
    C.kx = nc.dram_tensor("kx", [NH, 65, S], BF16, kind="Internal").ap()
    C.vx = nc.dram_tensor("vx", [NH, S, HD], BF16, kind="Internal").ap()
    C.oTd = nc.dram_tensor("oTd", [D, S], BF16, kind="Internal").ap()

    with ExitStack() as es:
        k = K(nc, es)
        C.ps = [es.enter_context(nc.psum_tensor(f"ps{i}", [128, 512], F32)) for i in range(8)]
        C.psb = [k.buf(f"ps{i}") for i in range(8)]
        C.ident = sb(nc, es, "ident", [128, 128], BF16)
        C.identf = sb(nc, es, "identf", [128, 128], F32)
        C.eps_t = sb(nc, es, "eps_t", [128, 1], F32)
        C.one_t = sb(nc, es, "one_t", [128, 1], F32)
        C.constb = k.buf("const", dma=True)
        block = es.enter_context(nc.Block())
        k.dma(k.sp, C.identf[:], C.k_ident, C.constb, writes=[C.constb])
        k.op(k.dve, lambda e: e.tensor_copy(out=C.ident[:], in_=C.identf[:]), reads=[C.constb], writes=[C.constb])
        k.op(k.dve, lambda e: e.memset(C.eps_t[:], EPS), writes=[C.constb])
        k.op(k.dve, lambda e: e.memset(C.one_t[:], 1.0), writes=[C.constb])
        keep = k.dnext
        k_release = k.release_dsems

        def release():
            k.dnext = keep
        k.release_dsems = release

        layers = sorted(set(p[1] for p in plan))
        phase_mods(k, C, layers)
        cur = C.x
        for (kind, l, i) in plan:
            if kind == "ffn":
                phase_ffn(k, C, l, i, cur, C.out)
            elif kind == "swa":
                phase_swa(k, C, l, cur, C.out)
            elif kind == "fox":
                phase_fox(k, C, l, cur, C.out)
            cur = C.out
        k.barrier()
    return nc


def emit_rstd_lnexp(k, C, ss, ssb, rs, rsb, n):
    k.op(k.act, lambda e: e.activation(out=rs[:], in_=ss[:], func=AF.Ln, bias=C.eps_t[:, 0:1], scale=1.0 / n),
         reads=[ssb, C.constb], writes=[rsb])
    k.op(k.act, lambda e: e.activation(out=rs[:], in_=rs[:], func=AF.Exp, scale=-0.5), reads=[rsb], writes=[rsb])


def emit_x_norm(k, C, xt, xb, p, ss, ssb, rs, rsb, hbf, hbfb, tmp, tmpb, a_t, a_b, sh_t, sh_b, hT, hTb, use_sqrt=False):
    k.op(k.dve, lambda e: e.memset(ss[p][:], 0.0), writes=[ssb[p]])
    for s in range(4):
        emit_norm_stats(k, C, xt[s][:], xb[s], ss[p], ssb[p], s, hbf[s % 2], hbfb[s % 2])
    emit_rstd_lnexp(k, C, ss[p], ssb[p], rs[p], rsb[p], D)
    for s in range(4):
        r = s % 2
        k.op(k.dve, lambda e, s=s, r=r: e.scalar_tensor_tensor(out=tmp[r][:], in0=xt[s][:], scalar=rs[p][:, s:s + 1], in1=a_t[:],
                                                                 op0=ALU.mult, op1=ALU.mult),
             reads=[xb[s], rsb[p], a_b], writes=[tmpb[r]])
        k.op(k.dve, lambda e, r=r: e.tensor_tensor(out=hbf[r][:], in0=tmp[r][:], in1=sh_t[:], op=ALU.add),
             reads=[tmpb[r], sh_b], writes=[hbfb[r]])
        emit_transposes(k, C, hbf[r], hbfb[r], hT, hTb[s], s, 6 + r)


def emit_qk_chunk(k, C, n, wt, wtb, col, hT, hTb, bones, sq, sqb, lnt, lntb, gcol, gb, dsts, dst_buf, pend, post=None):
    bank = n % 3
    r = n % 2
    ssbank = 3 + r

    def mm(e):
        for kc in range(KC):
            ins = e.matmul(C.ps[bank][:, :], lhsT=wt[:, kc, col:col + 128], rhs=hT[:, kc, :], start=(kc == 0), stop=(kc == KC - 1))
        return ins
    k.op(k.pe, mm, reads=[wtb] + hTb, writes=[C.psb[bank]])
    k.op(k.act, lambda e: e.activation(out=sq[r][:], in_=C.ps[bank][:, :], func=AF.Square), reads=[C.psb[bank]], writes=[sqb[r]])
    if pend:
        pend.pop()()

    def tail():
        k.op(k.pe, lambda e: e.matmul(C.ps[ssbank][:, :], lhsT=bones[:], rhs=sq[r][:], start=True, stop=True),
             reads=[sqb[r], C.constb], writes=[C.psb[ssbank]])
        k.op(k.act, lambda e: e.activation(out=lnt[r][:], in_=C.ps[ssbank][:, :], func=AF.Ln, bias=C.eps_t[:, 0:1], scale=1.0 / HD),
             reads=[C.psb[ssbank], C.constb], writes=[lntb[r]])
        k.op(k.act, lambda e: e.activation(out=lnt[r][:], in_=lnt[r][:], func=AF.Exp, scale=-0.5), reads=[lntb[r]], writes=[lntb[r]])
        for (dst_ap, sl) in dsts:
            k.op(k.dve, lambda e, dst_ap=dst_ap, sl=sl: e.scalar_tensor_tensor(out=dst_ap, in0=C.ps[bank][sl, :], scalar=gcol[sl, 0:1],
                                                                               in1=lnt[r][sl, :], op0=ALU.mult, op1=ALU.mult),
                 reads=[C.psb[bank], lntb[r], gb], writes=[dst_buf])
        if post is not None:
            post()
    pend.append(tail)


def load_gcol(k, C, es, name, src_1d, scale):
    nc = k.nc
    t = sb(nc, es, name, [128, 1], F32)
    b = k.buf(name, dma=True)
    v = src_1d.rearrange("(p o) -> p o", o=1)
    k.dma(k.sp, t[0:64, :], v, b, writes=[b])
    k.dma(k.sp, t[64:128, :], v, b, writes=[b])
    if scale != 1.0:
        k.op(k.dve, lambda e: e.tensor_scalar(out=t[:], in0=t[:], scalar1=float(scale), scalar2=None, op0=ALU.mult), reads=[b], writes=[b])
    return t, b


def make_bones(k, C, es):
    nc = k.nc
    bones = sb(nc, es, "bones", [128, 128], BF16)
    k.op(k.dve, lambda e: e.memset(bones[:], 0.0), writes=[C.constb])
    k.op(k.dve, lambda e: e.memset(bones[0:64, 0:64], 1.0), writes=[C.constb])
    k.op(k.dve, lambda e: e.memset(bones[64:128, 64:128], 1.0), writes=[C.constb])
    return bones


def phase_swa(k, C, l, src, dst):
    nc = k.nc
    S = C.S
    NT = S // 512
    jl = l // 2
    with ExitStack() as es:
        xs = [[sb(nc, es, f"xs{p}_{s}", [128, D], F32) for s in range(4)] for p in range(2)]
        xsb = [[k.buf(f"xs{p}_{s}", dma=True) for s in range(4)] for p in range(2)]
        (a_t, a_b), (sh_t, sh_b), (ga_t, ga_b) = mod_prologue(k, C, es, l, 1, 1.0, ga=(xs[1][0], xsb[1][0]))
        w_in = C.swa_w_in
        wq = sb(nc, es, "wq", [128, KC, 1024], BF16)
        wqb = k.buf("wq", dma=True)
        for kc in range(KC):
            k.dma(k.pool, wq[:, kc, :], w_in[jl, kc * 128:(kc + 1) * 128, 0:1024], wqb, writes=[wqb])
        wkd = sb(nc, es, "wkd", [128, KC, 512], BF16)
        wkdb = k.buf("wkd", dma=True)
        for j in range(4):
            for hf in range(2):
                k.dma(k.pool, wkd[:, :, j * 128 + hf * 64: j * 128 + hf * 64 + 64],
                      w_in[jl, :, 1024 + j * 64:1024 + (j + 1) * 64].rearrange("(kc p) n -> p kc n", p=128), wkdb, writes=[wkdb])
        wv = sb(nc, es, "wv", [128, KC, 256], BF16)
        wvb = k.buf("wv", dma=True)
        k.dma(k.pool, wv[:], w_in[jl, :, 1280:1536].rearrange("(kc p) n -> p kc n", p=128), wvb, writes=[wvb])
        wo = sb(nc, es, "wo", [128, 8, D], BF16)
        wob = k.buf("wo")
        stage = [xs[1][1], xs[1][2]]
        stage_b = [xsb[1][1], xsb[1][2]]
        cnt = 0
        for jp in range(2):
            for g in range(4):
                for hf in range(2):
                    head = 4 * (2 * jp + hf) + g
                    load_scaled_weight(k, C, stage, stage_b, cnt, wo[hf * 64:(hf + 1) * 64, jp * 4 + g, :], wob,
                                       C.swa_w_out[jl, head * 64:(head + 1) * 64, :], ga_t, ga_b, rows=64, p0=hf * 64)
                    cnt += 1
        gq, gqb = load_gcol(k, C, es, "gq", C.swa_q_g[jl, :], HD ** -0.5)
        gk, gkb = load_gcol(k, C, es, "gk", C.swa_k_g[jl, :], 1.0)
        bones = make_bones(k, C, es)
        EB = sb(nc, es, "EB", [128, 4096], F32)
        EBb = k.buf("EB", dma=True)
        k.dma(k.sp, EB[:], C.swa_bias, EBb, writes=[EBb])
        k.op(k.act, lambda e: e.activation(out=EB[:], in_=EB[:], func=AF.Exp), reads=[EBb], writes=[EBb])
        for s in range(4):
            k.dma(k.sp, xs[0][s][:], C.k_swamask[:, s * 1024:(s + 1) * 1024], xsb[0][s], writes=[xsb[0][s]])
            k.op(k.dve, lambda e, s=s: e.tensor_tensor(out=EB[:, s * 1024:(s + 1) * 1024], in0=EB[:, s * 1024:(s + 1) * 1024],
                                                       in1=xs[0][s][:], op=ALU.mult), reads=[EBb, xsb[0][s]], writes=[EBb])
        s16 = sb(nc, es, "s16", [128, NH], F32)
        ES = sb(nc, es, "ES", [128, NH, 128], F32)
        ESb = k.buf("ES", dma=True)
        k.dma(k.sp, s16[:], bcast_row(C.swa_sink[jl, :], NH), ESb, writes=[ESb])
        k.op(k.act, lambda e: e.activation(out=s16[:], in_=s16[:], func=AF.Exp), reads=[ESb], writes=[ESb])
        k.op(k.dve, lambda e: e.tensor_copy(out=ES[:], in_=s16[:].unsqueeze(2).to_broadcast([128, NH, 128])), reads=[ESb], writes=[ESb])

        vext = sb(nc, es, "vext", [128, 8, 4, 128], BF16)
        vextb = [k.buf(f"vext{i}") for i in range(8)]
        k.op(k.dve, lambda e: e.memset(vext[:], 1.0), writes=vextb)
        kTa = sb(nc, es, "kTa", [128, 4, 1024], BF16)
        kTb = sb(nc, es, "kTb", [128, 4, 1024], BF16)
        k.op(k.dve, lambda e: e.memset(kTa[:], 0.0), writes=[C.constb])
        k.op(k.dve, lambda e: e.memset(kTb[:], 0.0), writes=[C.constb])
        kTdb = [[k.buf(f"kTd{j}_{h}") for h in range(2)] for j in range(4)]
        qT = sb(nc, es, "qT", [128, 8, 512], BF16)
        qTb = [k.buf(f"qT{c}") for c in range(8)]
        hbf = [sb(nc, es, f"hbf{j}", [128, D], BF16) for j in range(2)]
        hbfb = [k.buf(f"hbf{j}") for j in range(2)]
        hT = sb(nc, es, "hT", [128, KC, 512], BF16)
        hTb = [k.buf(f"hT{j}") for j in range(4)]
        tmp = [sb(nc, es, f"tmp{j}", [128, D], F32) for j in range(2)]
        tmpb = [k.buf(f"tmp{j}") for j in range(2)]
        sq = [sb(nc, es, f"sq{j}", [128, 512], BF16) for j in range(2)]
        sqb = [k.buf(f"sq{j}") for j in range(2)]
        lnt = [sb(nc, es, f"lnt{j}", [128, 512], F32) for j in range(2)]
        lntb = [k.buf(f"lnt{j}") for j in range(2)]
        et = [sb(nc, es, f"et{j}", [128, 512], F32) for j in range(4)]
        etb = [k.buf(f"et{j}") for j in range(4)]
        pt = [sb(nc, es, f"pt{j}", [128, 512], BF16) for j in range(4)]
        ptb = [k.buf(f"pt{j}") for j in range(4)]
        rd = [sb(nc, es, f"rd{j}", [128, 512], F32) for j in range(2)]
        rdb = [k.buf(f"rd{j}") for j in range(2)]
        oT = [sb(nc, es, f"oT{j}", [128, 8, 128], BF16) for j in range(2)]
        oTb = [k.buf(f"oT{j}") for j in range(2)]
        ss = [sb(nc, es, f"ss{j}", [128, 4], F32) for j in range(2)]
        ssb = [k.buf(f"ss{j}") for j in range(2)]
        rs = [sb(nc, es, f"rs{j}", [128, 4], F32) for j in range(2)]
        rsb = [k.buf(f"rs{j}") for j in range(2)]

        def L(t):
            for s in range(4):
                r0 = t * 512 + s * 128
                k.dma(k.sp, xs[t % 2][s][:], src[r0:r0 + 128, :], xsb[t % 2][s], writes=[xsb[t % 2][s]])

        def sT_unit(t, s, j, u):
            b = 4 * t + s
            kbs = [(1, b % 8)] if b == 0 else [(0, (b - 1) % 8), (1, b % 8)]
            for (kbi, kslot) in kbs:
                bank = (u % 2) * 2 + kbi

                def mm(e, kslot=kslot, bank=bank):
                    for g in range(4):
                        head = 4 * j + g
                        c, hf = head // 2, head % 2
                        kT_ = kTa if hf == 0 else kTb
                        ins = e.matmul(C.ps[bank][:, g * 128:(g + 1) * 128],
                                       lhsT=kT_[:, j, kslot * 128:(kslot + 1) * 128],
                                       rhs=qT[:, c, s * 128:(s + 1) * 128], start=True, stop=True)
                    return ins
                k.op(k.pe, mm, reads=[kTdb[j][kslot // 4], qTb[2 * j], qTb[2 * j + 1], C.constb], writes=[C.psb[bank]])
            return kbs

        def rest_unit(t, s, j, u, kbs):
            b = 4 * t + s
            jp = j // 2
            AST = int(os.environ.get("ATT_STOP", "9"))
            if AST <= 0:
                return
            accb = 4 + u % 2
            num = slice(0, 64) if j % 2 == 0 else slice(64, 128)
            den = slice(64, 128) if j % 2 == 0 else slice(0, 64)
            for (kbi, kslot) in kbs:
                bank = (u % 2) * 2 + kbi
                r = bank
                k.op(k.act, lambda e, r=r, bank=bank: e.activation(out=et[r][:], in_=C.ps[bank][:, :], func=AF.Exp),
                     reads=[C.psb[bank]], writes=[etb[r]])
                off = (j * 2 + kbi) * 512
                k.op(k.dve, lambda e, r=r, off=off: e.tensor_tensor(out=pt[r][:], in0=et[r][:], in1=EB[:, off:off + 512], op=ALU.mult),
                     reads=[etb[r], EBb], writes=[ptb[r]])

            if AST <= 1:
                return

            def pv(e):
                for idx, (kbi, kslot) in enumerate(kbs):
                    r = (u % 2) * 2 + kbi
                    ins = e.matmul(C.ps[accb][:, :], lhsT=vext[:, kslot, j, :], rhs=pt[r][:],
                                   start=(idx == 0), stop=(idx == len(kbs) - 1))
                return ins
            k.op(k.pe, pv, reads=[ptb[(u % 2) * 2 + kbi] for (kbi, _) in kbs] + [vextb[ks] for (_, ks) in kbs], writes=[C.psb[accb]])
            if AST <= 2:
                return
            r2 = u % 2
            acc3 = C.ps[accb][:, :].rearrange("p (g q) -> p g q", g=4)
            rd3 = rd[r2][:].rearrange("p (g q) -> p g q", g=4)
            k.op(k.dve, lambda e: e.tensor_tensor(out=rd3[den], in0=acc3[den], in1=ES[den, j * 4:(j + 1) * 4, :], op=ALU.add),
                 reads=[C.psb[accb], ESb], writes=[rdb[r2]])
            k.op(k.dve, lambda e: e.reciprocal(out=rd3[den], in_=rd3[den]), reads=[rdb[r2]], writes=[rdb[r2]])
            k.op(k.dve, lambda e: e.tensor_tensor(out=oT[b % 2][num, jp * 4:(jp + 1) * 4, :], in0=acc3[num], in1=rd3[den], op=ALU.mult),
                 reads=[C.psb[accb], rdb[r2]], writes=[oTb[b % 2]])

        def outproj(t, s):
            b = 4 * t + s
            r0 = t * 512 + s * 128
            xt, xb = xs[t % 2][s], xsb[t % 2][s]
            for n in range(2):
                bank = 6 + n

                def mm(e, n=n, bank=bank):
                    for ci in range(8):
                        ins = e.matmul(C.ps[bank][:, :], lhsT=oT[b % 2][:, ci, :], rhs=wo[:, ci, n * 512:(n + 1) * 512],
                                       start=(ci == 0), stop=(ci == 7))
                    return ins
                k.op(k.pe, mm, reads=[oTb[b % 2], wob], writes=[C.psb[bank]])
                k.op(k.dve, lambda e, n=n, bank=bank: e.tensor_tensor(out=xt[:, n * 512:(n + 1) * 512], in0=xt[:, n * 512:(n + 1) * 512],
                                                                      in1=C.ps[bank][:, :], op=ALU.add),
                     reads=[xb, C.psb[bank]], writes=[xb])
            k.dma(k.sp, dst[r0:r0 + 128, :], xt[:], xb, reads=[xb])

        STOP = int(os.environ.get("SWA_STOP", "9"))
        L(0)
        ucount = 0
        for t in range(NT):
            if STOP <= 1:
                break
            if t + 1 < NT:
                L(t + 1)
            xt, xb = xs[t % 2], xsb[t % 2]
            emit_x_norm(k, C, xt, xb, t % 2, ss, ssb, rs, rsb, hbf, hbfb, tmp, tmpb, a_t, a_b, sh_t, sh_b, hT, hTb)
            pend = []
            n = 0
            if STOP <= 2:
                continue
            for c in range(8):
                emit_qk_chunk(k, C, n, wq, wqb, c * 128, hT, hTb, bones, sq, sqb, lnt, lntb, gq, gqb, [(qT[:, c, :], slice(0, 128))], qTb[c], pend)
                n += 1
            for j in range(4):
                c0 = (t % 2) * 512
                emit_qk_chunk(k, C, n, wkd, wkdb, j * 128, hT, hTb, bones, sq, sqb, lnt, lntb, gk, gkb,
                              [(kTa[0:64, j, c0:c0 + 512], slice(0, 64)), (kTb[64:128, j, c0:c0 + 512], slice(64, 128))],
                              kTdb[j][t % 2], pend)
                n += 1
            if STOP <= 3:
                while pend:
                    pend.pop()()
                continue
            for s in range(4):
                bank = n % 3
                n += 1
                rb = (4 * t + s) % 8

                def mm(e, s=s, bank=bank):
                    for kc in range(KC):
                        ins = e.matmul(C.ps[bank][:, 0:256], lhsT=hT[:, kc, s * 128:(s + 1) * 128], rhs=wv[:, kc, :],
                                       start=(kc == 0), stop=(kc == KC - 1))
                    return ins
                k.op(k.pe, mm, reads=[wvb, hTb[s]], writes=[C.psb[bank]])
                if pend:
                    pend.pop()()
                for j in range(4):
                    c0 = 0 if j % 2 == 0 else 64
                    k.op(k.act, lambda e, j=j, c0=c0, rb=rb, bank=bank: e.copy(out=vext[:, rb, j, c0:c0 + 64], in_=C.ps[bank][:, j * 64:(j + 1) * 64]),
                         reads=[C.psb[bank]], writes=[vextb[rb]])
            while pend:
                pend.pop()()
            if STOP <= 4:
                continue
            units = [(s, j) for s in range(4) for j in range(4)]
            kb_next = sT_unit(t, units[0][0], units[0][1], ucount)
            for ui, (s, j) in enumerate(units):
                u = ucount
                kbs = kb_next
                if ui + 1 < len(units):
                    kb_next = sT_unit(t, units[ui + 1][0], units[ui + 1][1], u + 1)
                rest_unit(t, s, j, u, kbs)
                ucount += 1
                if j == 3 and STOP > 5:
                    outproj(t, s)
        k.barrier()
    k.release_dsems()


def phase_fox(k, C, l, src, dst):
    nc = k.nc
    S = C.S
    NB = S // 128
    with ExitStack() as esl:
        negF = sb(nc, esl, "negF", [128, NB, NH], F32)
        negFb = k.buf("negF")
        fox_A(k, C, l, src, negF, negFb)
        fox_B(k, C, l, negF, negFb)
        fox_C(k, C, l, src, dst)


def fox_A(k, C, l, src, negF, negFb):
    nc = k.nc
    S = C.S
    NT = S // 512
    jl = l // 2
    with ExitStack() as es:
        xs = [[sb(nc, es, f"xs{p}_{s}", [128, D], F32) for s in range(4)] for p in range(2)]
        xsb = [[k.buf(f"xs{p}_{s}", dma=True) for s in range(4)] for p in range(2)]
        (a_t, a_b), (sh_t, sh_b), _ = mod_prologue(k, C, es, l, 1, 1.0, ga=(xs[1][0], xsb[1][0]))
        w_in = C.fox_w_in
        wts = []
        for nm, c0 in (("wq", 0), ("wk", 1024), ("wv", 2048)):
            wt = sb(nc, es, nm, [128, KC, 1024], BF16)
            wb = k.buf(nm, dma=True)
            for kc in range(KC):
                k.dma(k.pool, wt[:, kc, :], w_in[jl, kc * 128:(kc + 1) * 128, c0:c0 + 1024], wb, writes=[wb])
            wts.append((wt, wb))
        (wq, wqb), (wk, wkb), (wv, wvb) = wts
        wf = sb(nc, es, "wf", [128, KC, NH], BF16)
        wfb = k.buf("wf", dma=True)
        k.dma(k.pool, wf[:], w_in[jl, :, 3072:3088].rearrange("(kc p) n -> p kc n", p=128), wfb, writes=[wfb])
        gq, gqb = load_gcol(k, C, es, "gq", C.fox_q_g[jl, :], HD ** -0.5)
        gk, gkb = load_gcol(k, C, es, "gk", C.fox_k_g[jl, :], 1.0)
        bones = make_bones(k, C, es)
        negbf = sb(nc, es, "negbf", [NH, 1], F32)
        nbb = k.buf("negbf", dma=True)
        k.dma(k.sp, negbf[:], C.fox_b_f[jl, :].rearrange("(p o) -> p o", o=1), nbb, writes=[nbb])
        k.op(k.dve, lambda e: e.tensor_scalar(out=negbf[:], in0=negbf[:], scalar1=-1.0, scalar2=None, op0=ALU.mult), reads=[nbb], writes=[nbb])
        ones16 = sb(nc, es, "ones16", [NH, 512], F32)
        onesbf = sb(nc, es, "onesbf", [NH, 512], BF16)
        negI = sb(nc, es, "negI", [NH, NH], F32)
        cb2 = k.buf("fconst", dma=True)
        k.op(k.dve, lambda e: e.memset(ones16[:], 1.0), writes=[cb2])
        k.op(k.dve, lambda e: e.memset(onesbf[:], 1.0), writes=[cb2])
        k.op(k.dve, lambda e: e.tensor_scalar(out=negI[:], in0=C.identf[0:NH, 0:NH], scalar1=-1.0, scalar2=None, op0=ALU.mult),
             reads=[C.constb], writes=[cb2])

        hbf = [sb(nc, es, f"hbf{j}", [128, D], BF16) for j in range(2)]
        hbfb = [k.buf(f"hbf{j}") for j in range(2)]
        hT = sb(nc, es, "hT", [128, KC, 512], BF16)
        hTb = [k.buf(f"hT{j}") for j in range(4)]
        tmp = [sb(nc, es, f"tmp{j}", [128, D], F32) for j in range(2)]
        tmpb = [k.buf(f"tmp{j}") for j in range(2)]
        sq = [sb(nc, es, f"sq{j}", [128, 512], BF16) for j in range(2)]
        sqb = [k.buf(f"sq{j}") for j in range(2)]
        lnt = [sb(nc, es, f"lnt{j}", [128, 512], F32) for j in range(2)]
        lntb = [k.buf(f"lnt{j}") for j in range(2)]
        NQ = 4
        qn = [sb(nc, es, f"qn{j}", [128, 512], BF16) for j in range(NQ)]
        qnb = [k.buf(f"qn{j}", dma=True) for j in range(NQ)]
        vt = [sb(nc, es, f"vt{j}", [128, 512], BF16) for j in range(2)]
        vtb = [k.buf(f"vt{j}", dma=True) for j in range(2)]
        lf = [sb(nc, es, f"lf{j}", [NH, 512], F32) for j in range(2)]
        lfb = [k.buf(f"lf{j}") for j in range(2)]
        FT = [sb(nc, es, f"FT{j}", [NH, 512], F32) for j in range(2)]
        FTb = [k.buf(f"FT{j}") for j in range(2)]
        Rb = [sb(nc, es, f"Rb{j}", [NH, 512], BF16) for j in range(2)]
        Rbb = [k.buf(f"Rb{j}", dma=True) for j in range(2)]
        ss = [sb(nc, es, f"ss{j}", [128, 4], F32) for j in range(2)]
        ssb = [k.buf(f"ss{j}") for j in range(2)]
        rs = [sb(nc, es, f"rs{j}", [128, 4], F32) for j in range(2)]
        rsb = [k.buf(f"rs{j}") for j in range(2)]

        def L(t):
            for s in range(4):
                r0 = t * 512 + s * 128
                k.dma(k.sp, xs[t % 2][s][:], src[r0:r0 + 128, :], xsb[t % 2][s], writes=[xsb[t % 2][s]])

        L(0)
        n = 0
        qi = 0
        for t in range(NT):
            if t + 1 < NT:
                L(t + 1)
            c0 = t * 512
            emit_x_norm(k, C, xs[t % 2], xsb[t % 2], t % 2, ss, ssb, rs, rsb, hbf, hbfb, tmp, tmpb, a_t, a_b, sh_t, sh_b, hT, hTb)
            pend = []
            for (wt, wb, gcol, gb, dram) in ((wq, wqb, gq, gqb, C.qx), (wk, wkb, gk, gkb, C.kx)):
                for c in range(8):
                    r = qi % NQ
                    qi += 1

                    def post(r=r, c=c, dram=dram):
                        for hf in range(2):
                            k.dma(k.sp, dram[2 * c + hf, 0:64, c0:c0 + 512], qn[r][hf * 64:(hf + 1) * 64, :], qnb[r], reads=[qnb[r]])
                    emit_qk_chunk(k, C, n, wt, wb, c * 128, hT, hTb, bones, sq, sqb, lnt, lntb, gcol, gb,
                                  [(qn[r][:], slice(0, 128))], qnb[r], pend, post=post)
                    n += 1
            fb = 5

            def mmf(e):
                for kc in range(KC):
                    ins = e.matmul(C.ps[fb][0:NH, :], lhsT=wf[:, kc, :], rhs=hT[:, kc, :], start=(kc == 0), stop=(kc == KC - 1))
                return ins
            k.op(k.pe, mmf, reads=[wfb] + hTb, writes=[C.psb[fb]])
            pend.pop()()
            p = t % 2
            k.op(k.act, lambda e, p=p: e.activation(out=lf[p][:], in_=C.ps[fb][0:NH, :], func=AF.Exp, bias=negbf[:, 0:1], scale=-1.0),
                 reads=[C.psb[fb], nbb], writes=[lfb[p]])
            k.op(k.act, lambda e, p=p: e.activation(out=lf[p][:], in_=lf[p][:], func=AF.Ln, bias=C.one_t[0:NH, 0:1], scale=1.0),
                 reads=[lfb[p], C.constb], writes=[lfb[p]])
            init = 0.0 if t == 0 else FT[1 - p][:, 511:512]
            k.op(k.dve, lambda e, p=p, init=init: e.tensor_tensor_scan(out=FT[p][:], data0=ones16[:], data1=lf[p][:], initial=init,
                                                                      op0=ALU.mult, op1=ALU.subtract),
                 reads=[lfb[p], cb2, FTb[1 - p]], writes=[FTb[p]])
            k.op(k.dve, lambda e, p=p: e.tensor_copy(out=Rb[p][:], in_=FT[p][:]), reads=[FTb[p]], writes=[Rbb[p]])
            k.dma(k.sp, C.qx[:, 64, c0:c0 + 512], Rb[p][:], Rbb[p], reads=[Rbb[p]])
            k.dma(k.sp, C.kx[:, 64, c0:c0 + 512], onesbf[:], cb2, reads=[cb2])
            for s in range(4):
                def mmt(e, s=s, p=p):
                    return e.matmul(C.ps[fb][:, 16 + s * 16:32 + s * 16], lhsT=FT[p][:, s * 128:(s + 1) * 128], rhs=negI[:, :], start=True, stop=True)
                k.op(k.pe, mmt, reads=[FTb[p], cb2], writes=[C.psb[fb]])
            k.op(k.act, lambda e: e.copy(out=negF[:, 4 * t:4 * t + 4, :], in_=C.ps[fb][:, 16:80].rearrange("p (s h) -> p s h", s=4)),
                 reads=[C.psb[fb]], writes=[negFb])
            for s in range(4):
                r0 = t * 512 + s * 128
                for n2 in range(2):
                    bank = n % 3
                    n += 1
                    vr = (2 * s + n2) % 2

                    def mm(e, s=s, n2=n2, bank=bank):
                        for kc in range(KC):
                            ins = e.matmul(C.ps[bank][:, :], lhsT=hT[:, kc, s * 128:(s + 1) * 128], rhs=wv[:, kc, n2 * 512:(n2 + 1) * 512],
                                           start=(kc == 0), stop=(kc == KC - 1))
                        return ins
                    k.op(k.pe, mm, reads=[wvb, hTb[s]], writes=[C.psb[bank]])
                    if pend:
                        pend.pop()()
                    k.op(k.act, lambda e, vr=vr, bank=bank: e.copy(out=vt[vr][:], in_=C.ps[bank][:, :]), reads=[C.psb[bank]], writes=[vtb[vr]])
                    k.dma(k.sp, C.vx[n2 * 8:(n2 + 1) * 8, r0:r0 + 128, :].rearrange("h t d -> t h d"),
                          vt[vr][:].rearrange("p (h d) -> p h d", h=8), vtb[vr], reads=[vtb[vr]])
            while pend:
                pend.pop()()
        k.barrier()
    k.release_dsems()


def fox_B(k, C, l, negF, negFb):
    nc = k.nc
    S = C.S
    NT = S // 512
    NB = S // 128
    with ExitStack() as es:
        qs = [sb(nc, es, f"qs{j}", [65, S], BF16) for j in range(2)]
        ks = [sb(nc, es, f"ks{j}", [65, S], BF16) for j in range(2)]
        vs = [sb(nc, es, f"vs{j}", [128, NB, 128], BF16) for j in range(2)]
        qsb = [k.buf(f"qs{j}", dma=True) for j in range(2)]
        ksb = [k.buf(f"ks{j}", dma=True) for j in range(2)]
        vsb = [k.buf(f"vs{j}", dma=True) for j in range(2)]
        k.op(k.dve, lambda e: e.memset(vs[0][:, :, 64:128], 1.0), writes=[vsb[0]])
        k.op(k.dve, lambda e: e.memset(vs[1][:, :, 0:64], 1.0), writes=[vsb[1]])
        negtri = sb(nc, es, "negtri", [128, 128], F32)
        ntb = k.buf("negtri", dma=True)
        k.dma(k.sp, negtri[:], C.k_negtri, ntb, writes=[ntb])
        NP = 3
        pT = [sb(nc, es, f"pT{j}", [128, 512], BF16) for j in range(NP)]
        pTb = [k.buf(f"pT{j}") for j in range(NP)]
        rdn = [sb(nc, es, f"rdn{j}", [128, 512], F32) for j in range(2)]
        rdnb = [k.buf(f"rdn{j}") for j in range(2)]
        oTs = [sb(nc, es, f"oTs{j}", [128, 512], BF16) for j in range(2)]
        oTsb = [k.buf(f"oTs{j}", dma=True) for j in range(2)]

        def load(h):
            hb = h % 2
            k.dma(k.sp, qs[hb][:], C.qx[h, :, :], qsb[hb], writes=[qsb[hb]])
            k.dma(k.sp, ks[hb][:], C.kx[h, :, :], ksb[hb], writes=[ksb[hb]])
            vc = 0 if hb == 0 else 64
            k.dma(k.sp, vs[hb][:, :, vc:vc + 64], C.vx[h, :, :].rearrange("(b p) d -> p b d", p=128), vsb[hb], writes=[vsb[hb]])

        load(0)
        tcount = 0
        ucount = 0
        for h in range(NH):
            hb = h % 2
            if h + 1 < NH:
                load(h + 1)
            num = slice(0, 64) if hb == 0 else slice(64, 128)
            den = slice(64, 128) if hb == 0 else slice(0, 64)
            units = [(t, kb) for t in range(NT) for kb in range(4 * t + 4)]

            def qk(ui):
                t, kb = units[ui]
                u = ucount + ui
                bank = u % NP
                c0 = 128 * (kb - 4 * t) if kb >= 4 * t else 0
                k.op(k.pe, lambda e: e.matmul(C.ps[bank][:, c0:512], lhsT=ks[hb][:, kb * 128:(kb + 1) * 128],
                                              rhs=qs[hb][:, t * 512 + c0:(t + 1) * 512], start=True, stop=True),
                     reads=[ksb[hb], qsb[hb]], writes=[C.psb[bank]])

            qk(0)
            if len(units) > 1:
                qk(1)
            for ui, (t, kb) in enumerate(units):
                u = ucount + ui
                bank = u % NP
                r = u % NP
                diag = kb >= 4 * t
                c0 = 128 * (kb - 4 * t) if diag else 0
                accb = 4 + (tcount + t) % 2
                if ui + 2 < len(units):
                    qk(ui + 2)
                if diag:
                    k.op(k.dve, lambda e: e.tensor_tensor(out=C.ps[bank][:, c0:c0 + 128], in0=C.ps[bank][:, c0:c0 + 128], in1=negtri[:], op=ALU.add),
                         reads=[C.psb[bank], ntb], writes=[C.psb[bank]])
                k.op(k.act, lambda e: e.activation(out=pT[r][:, c0:512], in_=C.ps[bank][:, c0:512], func=AF.Exp, bias=negF[:, kb, h:h + 1], scale=1.0),
                     reads=[C.psb[bank], negFb], writes=[pTb[r]])
                last = (kb == 4 * t + 3)
                k.op(k.pe, lambda e: e.matmul(C.ps[accb][:, c0:512], lhsT=vs[hb][:, kb, :], rhs=pT[r][:, c0:512], start=(kb == 0), stop=last),
                     reads=[pTb[r], vsb[hb]], writes=[C.psb[accb]])
                if last:
                    r2 = (tcount + t) % 2
                    k.op(k.dve, lambda e: e.reciprocal(out=rdn[r2][den, :], in_=C.ps[accb][den, :]), reads=[C.psb[accb]], writes=[rdnb[r2]])
                    k.op(k.dve, lambda e: e.tensor_tensor(out=oTs[r2][num, :], in0=C.ps[accb][num, :], in1=rdn[r2][den, :], op=ALU.mult),
                         reads=[C.psb[accb], rdnb[r2]], writes=[oTsb[r2]])
                    k.dma(k.sp, C.oTd[h * 64:(h + 1) * 64, t * 512:(t + 1) * 512], oTs[r2][num, :], oTsb[r2], reads=[oTsb[r2]])
            ucount += len(units)
            tcount += NT
        k.barrier()
    k.release_dsems()


def fox_C(k, C, l, src, dst):
    nc = k.nc
    S = C.S
    NT = S // 512
    jl = l // 2
    with ExitStack() as es:
        xs = [[sb(nc, es, f"xs{p}_{s}", [128, D], F32) for s in range(4)] for p in range(2)]
        xsb = [[k.buf(f"xs{p}_{s}", dma=True) for s in range(4)] for p in range(2)]
        _, _, (ga_t, ga_b) = mod_prologue(k, C, es, l, 1, 1.0, ga=(xs[1][0], xsb[1][0]))
        wo = sb(nc, es, "wo", [128, 8, D], BF16)
        wob = k.buf("wo")
        stage = [xs[1][1], xs[1][2]]
        stage_b = [xsb[1][1], xsb[1][2]]
        for c in range(8):
            load_scaled_weight(k, C, stage, stage_b, c, wo[:, c, :], wob, C.fox_w_out[jl, c * 128:(c + 1) * 128, :], ga_t, ga_b)
        oTt = [sb(nc, es, f"oTt{j}", [128, 8, 512], BF16) for j in range(2)]
        oTtb = [k.buf(f"oTt{j}", dma=True) for j in range(2)]

        def L(t):
            k.dma(k.sp, oTt[t % 2][:], C.oTd[:, t * 512:(t + 1) * 512].rearrange("(c p) t -> p c t", p=128), oTtb[t % 2], writes=[oTtb[t % 2]])
            for s in range(4):
                r0 = t * 512 + s * 128
                k.dma(k.sp, xs[t % 2][s][:], src[r0:r0 + 128, :], xsb[t % 2][s], writes=[xsb[t % 2][s]])

        L(0)
        for t in range(NT):
            if t + 1 < NT:
                L(t + 1)
            for s in range(4):
                r0 = t * 512 + s * 128
                xt, xb = xs[t % 2][s], xsb[t % 2][s]
                for n in range(2):
                    bank = (2 * s + n) % 4

                    def mm(e, s=s, n=n, bank=bank):
                        for c in range(8):
                            ins = e.matmul(C.ps[bank][:, :], lhsT=oTt[t % 2][:, c, s * 128:(s + 1) * 128], rhs=wo[:, c, n * 512:(n + 1) * 512],
                                           start=(c == 0), stop=(c == 7))
                        return ins
                    k.op(k.pe, mm, reads=[oTtb[t % 2], wob], writes=[C.psb[bank]])
                    k.op(k.dve, lambda e, xt=xt, n=n, bank=bank: e.tensor_tensor(out=xt[:, n * 512:(n + 1) * 512], in0=xt[:, n * 512:(n + 1) * 512],
                                                                                in1=C.ps[bank][:, :], op=ALU.add),
                         reads=[xb, C.psb[bank]], writes=[xb])
                k.dma(k.sp, dst[r0:r0 + 128, :], xt[:], xb, reads=[xb])
        k.barrier()
    k.release_dsems()


def rel_bucket_np(dist):
    n = np.maximum(dist, 0)
    max_exact = 16
    nf = np.maximum(n, 1).astype(np.float32)
    large = max_exact + (np.log(nf / max_exact) / math.log(128 / max_exact) * (32 - max_exact)).astype(np.int32)
    large = np.minimum(large, 31)
    return np.where(n < max_exact, n, large)


def host_constants(rel_bias):
    kk = np.arange(128)[:, None]
    q = np.arange(128)[None, :]
    bias = np.zeros((128, 4, 2, 4, 128), np.float32)
    mask = np.zeros((128, 4, 2, 4, 128), np.float32)
    for kb in range(2):
        dist = q - kk + (128 if kb == 0 else 0)
        valid = (dist >= 0) & (dist < 128)
        idx = rel_bucket_np(dist)
        for j in range(4):
            for g in range(4):
                bias[:, j, kb, g, :] = rel_bias[idx, 4 * j + g]
                mask[:, j, kb, g, :] = valid
    negtri = np.where(kk <= q, 0.0, NEG).astype(np.float32)
    return dict(swa_bias=bias.reshape(128, 4096), k_swamask=mask.reshape(128, 4096),
                k_negtri=negtri, k_ident=np.eye(128, dtype=np.float32))


FULL_PLAN = []
for _l in range(DEPTH):
    FULL_PLAN += [("ffn", _l, 0), ("swa" if _l % 2 == 0 else "fox", _l, 1), ("ffn", _l, 2)]


def make_in_maps(inputs, n_cores):
    consts = host_constants(np.asarray(inputs["rel_bias"], np.float32))
    shared = {kname: np.ascontiguousarray(np.asarray(inputs[kname], np.float32)) for kname in
              ("ada_w", "ada_b", "norm_g", "ffn_w13", "ffn_w2", "swa_w_in", "swa_w_out", "swa_q_g", "swa_k_g",
               "swa_sink", "fox_w_in", "fox_w_out", "fox_b_f", "fox_q_g", "fox_k_g")}
    shared.update(consts)
    x = np.asarray(inputs["x"], np.float32)
    c = np.asarray(inputs["c"], np.float32)
    maps = []
    for b in range(n_cores):
        m = dict(shared)
        m["x"] = np.ascontiguousarray(x[b])
        m["c"] = np.ascontiguousarray(c[b])
        maps.append(m)
    return maps


def kernel(**inputs):
    x = np.asarray(inputs["x"])
    B, S, _ = x.shape
    nc = build(S, FULL_PLAN)
    maps = make_in_maps(inputs, B)
    res = run_bass_kernel_spmd(nc, maps, core_ids=list(range(B)))
    return np.stack([np.asarray(r["out"], np.float32) for r in res.results], axis=0)
```

```python
import math
import os
from contextlib import ExitStack

import numpy as np
import concourse.bass as bass
import concourse.mybir as mybir
from concourse.bass_utils import run_bass_kernel_spmd

F32 = mybir.dt.float32
BF16 = mybir.dt.bfloat16
AF = mybir.ActivationFunctionType
ALU = mybir.AluOpType

D = 1024
DFF = 2816
HD = 64
NH = 16
KC = D // 128
FC = DFF // 128
EPS = 1e-6
DEPTH = 4
NEG = -30000.0
QK_DEPTH = 2


class Eng:
    def __init__(self, name, h, sem):
        self.name, self.h, self.sem = name, h, sem
        self.count = 0
        self.waited = {}

    def wait(self, sem, val):
        k = id(sem)
        if self.waited.get(k, 0) >= val:
            return
        self.h.wait_ge(sem, val)
        self.waited[k] = val


class DS:
    def __init__(self, sem):
        self.sem = sem
        self.count = 0


class Buf:
    __slots__ = ("name", "w", "r", "ds")

    def __init__(self, name, ds=None):
        self.name = name
        self.w = {}
        self.r = {}
        self.ds = ds


class K:
    def __init__(self, nc, es, n_dsem=48):
        self.nc = nc
        mk = lambda n: es.enter_context(nc.semaphore(n))
        self.pe = Eng("pe", nc.tensor, mk("s_pe"))
        self.act = Eng("act", nc.scalar, mk("s_act"))
        self.dve = Eng("dve", nc.vector, mk("s_dve"))
        self.pool = Eng("pool", nc.gpsimd, mk("s_pool"))
        self.sp = Eng("sp", nc.sync, mk("s_sp"))
        self.bar_sem = mk("s_bar")
        self.bar_count = 0
        self.dpool = [DS(mk(f"s_d{i}")) for i in range(n_dsem)]
        self.dnext = 0
        self.engs = [self.pe, self.act, self.dve, self.pool]

    def buf(self, name, dma=False):
        ds = None
        if dma:
            ds = self.dpool[self.dnext]
            self.dnext += 1
        return Buf(name, ds)

    def release_dsems(self):
        self.dnext = 0

    def _deps(self, eng, reads, writes, skip_ds=None):
        for b in reads:
            for (sem, val, src) in b.w.values():
                eng.wait(sem, val)
        for b in writes:
            for (sem, val, src) in b.w.values():
                if src is eng:
                    continue
                if skip_ds is not None and sem is skip_ds.sem:
                    continue
                eng.wait(sem, val)
            for (sem, val, src) in b.r.values():
                if src is eng:
                    continue
                eng.wait(sem, val)

    def op(self, eng, fn, reads=(), writes=()):
        self._deps(eng, reads, writes)
        ins = fn(eng.h)
        eng.count += 1
        ins.then_inc(eng.sem, 1)
        tok = (eng.sem, eng.count, eng)
        k = id(eng.sem)
        for b in reads:
            b.r[k] = tok
        for b in writes:
            b.w = {k: tok}
            b.r = {}
        return ins

    def dma(self, q, out, in_, slot, reads=(), writes=(), **kw):
        ds = slot.ds
        self._deps(q, reads, writes, skip_ds=ds)
        ins = q.h.dma_start(out=out, in_=in_, **kw)
        ds.count += 16
        ins.then_inc(ds.sem, 16)
        tok = (ds.sem, ds.count, None)
        k = id(ds.sem)
        for b in reads:
            b.r[k] = tok
        for b in writes:
            if k in b.w:
                b.w[k] = tok
            else:
                b.w = {k: tok}
                b.r = {}
        return ins

    def barrier(self):
        sp = self.sp
        for e in self.engs:
            if e.count:
                sp.wait(e.sem, e.count)
        for ds in self.dpool:
            if ds.count:
                sp.wait(ds.sem, ds.count)
        self.bar_count += 1
        sp.h.sem_inc(self.bar_sem, 1)
        for e in self.engs:
            e.h.wait_ge(self.bar_sem, self.bar_count)


class Ctx:
    _nb = 0
    _busy = None

    def nb(self):
        if self._busy is None:
            self._busy = [False] * 8
        for i in range(8):
            b = (self._nb + i) % 8
            if not self._busy[b]:
                self._busy[b] = True
                self._nb = (b + 1) % 8
                return b
        raise RuntimeError("no free PSUM bank")

    def fb(self, b):
        self._busy[b] = False


_uid = [0]


def sb(nc, es, name, shape, dt):
    _uid[0] += 1
    return es.enter_context(nc.sbuf_tensor(f"{name}_{_uid[0]}", shape, dt))


def bcast_row(dram_ap_1d, n):
    return dram_ap_1d.partition_broadcast(128)


def load_bcast(k, C, es, name, src_1d, n=D):
    nc = k.nc
    t = sb(nc, es, name, [128, n], F32)
    b = k.buf(name, dma=True)
    k.dma(k.sp, t[:], bcast_row(src_1d, n), b, writes=[b])
    return t, b


def emit_norm_stats(k, C, xt, xb, ss, ssb, col, junk, junkb):
    k.op(k.act, lambda e: e.activation(out=junk[:], in_=xt, func=AF.Square, accum_out=ss[:, col:col + 1]),
         reads=[xb], writes=[ssb, junkb])


def phase_mods(k, C, layers):
    nc = k.nc
    with ExitStack() as es:
        c_sb = sb(nc, es, "c_sb", [128, KC], F32)
        cact = sb(nc, es, "cact", [128, KC], F32)
        cb = k.buf("c", dma=True)
        cab = k.buf("cact")
        k.dma(k.sp, c_sb[:], C.c.rearrange("(kc p) -> p kc", p=128), cb, writes=[cb],
              allow_slow_non_contiguous=True)
        k.op(k.act, lambda e: e.activation(out=cact[:], in_=c_sb[:], func=AF.Silu), reads=[cb], writes=[cab])
        NR = 3
        wt = [sb(nc, es, f"adaw{i}", [128, KC, 512], F32) for i in range(NR)]
        wb = [k.buf(f"adaw{i}", dma=True) for i in range(NR)]
        bt = [sb(nc, es, f"adab{i}", [1, 512], F32) for i in range(NR)]
        bb = [k.buf(f"adab{i}", dma=True) for i in range(NR)]
        ot = [sb(nc, es, f"modo{i}", [1, 512], F32) for i in range(NR)]
        ob = [k.buf(f"modo{i}", dma=True) for i in range(NR)]
        n = 0
        for l in layers:
            for ct in range(9 * D // 512):
                s = n % NR
                pb = n % 2
                k.dma(k.sp, wt[s][:], C.ada_w[l, :, ct * 512:(ct + 1) * 512].rearrange("(kc p) n -> p kc n", p=128),
                      wb[s], writes=[wb[s]])
                k.dma(k.sp, bt[s][:], C.ada_b[l:l + 1, ct * 512:(ct + 1) * 512], bb[s], writes=[bb[s]])

                def mm(e, s=s, pb=pb):
                    for kc in range(KC):
                        ins = e.matmul(C.ps[pb][0:1, :], lhsT=cact[:, kc:kc + 1], rhs=wt[s][:, kc, :],
                                       start=(kc == 0), stop=(kc == KC - 1))
                    return ins
                k.op(k.pe, mm, reads=[cab, wb[s]], writes=[C.psb[pb]])
                k.op(k.dve, lambda e, s=s, pb=pb: e.tensor_tensor(out=ot[s][:], in0=C.ps[pb][0:1, :], in1=bt[s][:], op=ALU.add),
                     reads=[C.psb[pb], bb[s]], writes=[ob[s]])
                k.dma(k.sp, C.modv[l:l + 1, ct * 512:(ct + 1) * 512], ot[s][:], ob[s], reads=[ob[s]])
                n += 1
        k.barrier()
    k.release_dsems()


def mod_prologue(k, C, es, l, i, gate_scale, ga=None):
    nc = k.nc
    a_t, a_b = load_bcast(k, C, es, "a_b", C.modv[l, (i * 3 + 1) * D:(i * 3 + 2) * D])
    sh_t, sh_b = load_bcast(k, C, es, "sh_b", C.norm_g[l, i, :])
    if ga is None:
        ga_t, ga_b = load_bcast(k, C, es, "ga_b", C.modv[l, (i * 3 + 2) * D:(i * 3 + 3) * D])
    else:
        ga_t, ga_b = ga
        k.dma(k.sp, ga_t[:], bcast_row(C.modv[l, (i * 3 + 2) * D:(i * 3 + 3) * D], D), ga_b, writes=[ga_b])
    k.op(k.dve, lambda e: e.scalar_tensor_tensor(out=a_t[:], in0=a_t[:], scalar=1.0, in1=sh_t[:], op0=ALU.add, op1=ALU.mult),
         reads=[a_b, sh_b], writes=[a_b])
    k.op(k.dve, lambda e: e.tensor_scalar(out=ga_t[:], in0=ga_t[:], scalar1=float(gate_scale), scalar2=None, op0=ALU.mult),
         reads=[ga_b], writes=[ga_b])
    k.dma(k.sp, sh_t[:], bcast_row(C.modv[l, (i * 3 + 0) * D:(i * 3 + 1) * D], D), sh_b, writes=[sh_b])
    return (a_t, a_b), (sh_t, sh_b), (ga_t, ga_b)


def load_scaled_weight(k, C, stage, stage_b, cnt, dst_ap, dst_buf, src_ap, ga_t, ga_b, rows=128, p0=0):
    s = cnt % len(stage)
    k.dma(k.sp, stage[s][p0:p0 + rows, :], src_ap, stage_b[s], writes=[stage_b[s]])
    k.op(k.dve, lambda e: e.tensor_tensor(out=dst_ap, in0=stage[s][p0:p0 + rows, :], in1=ga_t[p0:p0 + rows, :], op=ALU.mult),
         reads=[stage_b[s], ga_b], writes=[dst_buf])


def emit_transposes(k, C, h_t, h_b, hT, hT_b, s, tpi):
    auto = tpi is None
    if auto:
        tpi = C.nb()
    tpb = C.ps[tpi][:].bitcast(BF16)

    def tr(e):
        for kc in range(KC):
            ins = e.transpose(tpb[:, kc * 128:(kc + 1) * 128], h_t[:, kc * 128:(kc + 1) * 128], C.ident[:])
        return ins
    k.op(k.pe, tr, reads=[h_b, C.constb], writes=[C.psb[tpi]])
    k.op(k.act, lambda e: e.copy(out=hT[:, :, s * 128:(s + 1) * 128], in_=tpb.rearrange("p (k t) -> p k t", k=KC)),
         reads=[C.psb[tpi]], writes=[hT_b])
    if auto:
        C.fb(tpi)


def phase_ffn(k, C, l, i, src, dst):
    nc = k.nc
    S = C.S
    NT = S // 512
    fi = i // 2
    with ExitStack() as es:
        xin = [sb(nc, es, f"xin{j}", [128, D], F32) for j in range(4)]
        xinb = [k.buf(f"xin{j}", dma=True) for j in range(4)]
        xres = [sb(nc, es, f"xres{j}", [128, D], F32) for j in range(2)]
        xresb = [k.buf(f"xres{j}", dma=True) for j in range(2)]
        (a_t, a_b), (sh_t, sh_b), (ga_t, ga_b) = mod_prologue(k, C, es, l, i, 0.5, ga=(xin[0], xinb[0]))
        w13 = sb(nc, es, "w13", [128, KC, 2 * DFF], BF16)
        w2 = sb(nc, es, "w2", [128, FC, D], BF16)
        w13b = k.buf("w13", dma=True)
        w2b = [k.buf(f"w2_{j}") for j in range(FC)]
        for kc in range(KC):
            k.dma(k.pool, w13[:, kc, :], C.ffn_w13[l, fi, kc * 128:(kc + 1) * 128, :], w13b, writes=[w13b])
        stage = xres
        stage_b = xresb
        for j in range(FC):
            load_scaled_weight(k, C, stage, stage_b, j, w2[:, j, :], w2b[j], C.ffn_w2[l, fi, j * 128:(j + 1) * 128, :], ga_t, ga_b)

        hbf = [sb(nc, es, f"hbf{j}", [128, D], BF16) for j in range(4)]
        hbfb = [k.buf(f"hbf{j}") for j in range(4)]
        hT = sb(nc, es, "hT", [128, KC, 512], BF16)
        hTb = [k.buf(f"hT{j}") for j in range(4)]
        actT = sb(nc, es, "actT", [128, FC, 512], BF16)
        actb = [k.buf(f"act{j}") for j in range(FC)]
        sg = [sb(nc, es, f"sg{j}", [128, 512], F32) for j in range(2)]
        sgb = [k.buf(f"sg{j}") for j in range(2)]
        ss = [sb(nc, es, f"ss{j}", [128, 4], F32) for j in range(2)]
        ssb = [k.buf(f"ss{j}") for j in range(2)]
        rs = [sb(nc, es, f"rs{j}", [128, 4], F32) for j in range(2)]
        rsb = [k.buf(f"rs{j}") for j in range(2)]

        def L(t):
            for s in range(4):
                r0 = t * 512 + s * 128
                k.dma(k.sp, xin[s][:], src[r0:r0 + 128, :], xinb[s], writes=[xinb[s]])

        def N(t):
            p = t % 2
            k.op(k.dve, lambda e: e.memset(ss[p][:], 0.0), writes=[ssb[p]])
            for s in range(4):
                emit_norm_stats(k, C, xin[s][:], xinb[s], ss[p], ssb[p], s, hbf[s], hbfb[s])
            k.op(k.act, lambda e: e.activation(out=rs[p][:], in_=ss[p][:], func=AF.Sqrt, bias=C.eps_t[:, 0:1], scale=1.0 / D),
                 reads=[ssb[p], C.constb], writes=[rsb[p]])
            k.op(k.dve, lambda e: e.reciprocal(out=rs[p][:], in_=rs[p][:]), reads=[rsb[p]], writes=[rsb[p]])
            for s in range(4):
                k.op(k.dve, lambda e, s=s: e.scalar_tensor_tensor(out=xin[s][:], in0=xin[s][:], scalar=rs[p][:, s:s + 1], in1=a_t[:],
                                                                 op0=ALU.mult, op1=ALU.mult),
                     reads=[xinb[s], rsb[p], a_b], writes=[xinb[s]])
                k.op(k.dve, lambda e, s=s: e.tensor_tensor(out=hbf[s][:], in0=xin[s][:], in1=sh_t[:], op=ALU.add),
                     reads=[xinb[s], sh_b], writes=[hbfb[s]])

        def T(t):
            for s in range(4):
                emit_transposes(k, C, hbf[s], hbfb[s], hT, hTb[s], s, 6 + (s % 2))

        def U(t, mid=None):
            for j in range(FC):
                pb = j % 2
                for half, bank in ((0, pb), (1, 2 + pb)):
                    col = half * DFF + j * 128

                    def mm(e, col=col, bank=bank):
                        for kc in range(KC):
                            ins = e.matmul(C.ps[bank][:, :], lhsT=w13[:, kc, col:col + 128], rhs=hT[:, kc, :],
                                           start=(kc == 0), stop=(kc == KC - 1))
                        return ins
                    k.op(k.pe, mm, reads=[w13b] + hTb, writes=[C.psb[bank]])
                k.op(k.act, lambda e, pb=pb: e.activation(out=sg[pb][:], in_=C.ps[pb][:, :], func=AF.Silu),
                     reads=[C.psb[pb]], writes=[sgb[pb]])
                k.op(k.dve, lambda e, pb=pb, j=j: e.tensor_tensor(out=actT[:, j, :], in0=sg[pb][:], in1=C.ps[2 + pb][:, :], op=ALU.mult),
                     reads=[sgb[pb], C.psb[2 + pb]], writes=[actb[j]])
                if mid is not None and j == 5:
                    mid()

        def Dn(t):
            for s in range(4):
                r0 = t * 512 + s * 128
                xs = s % 2
                k.dma(k.sp, xres[xs][:], src[r0:r0 + 128, :], xresb[xs], writes=[xresb[xs]])
                for n in range(2):
                    bank = 4 + (2 * s + n) % 2

                    def mm(e, s=s, n=n, bank=bank):
                        for j in range(FC):
                            ins = e.matmul(C.ps[bank][:, :], lhsT=actT[:, j, s * 128:(s + 1) * 128], rhs=w2[:, j, n * 512:(n + 1) * 512],
                                           start=(j == 0), stop=(j == FC - 1))
                        return ins
                    k.op(k.pe, mm, reads=actb + w2b, writes=[C.psb[bank]])
                    k.op(k.dve, lambda e, xs=xs, n=n, bank=bank: e.tensor_tensor(out=xres[xs][:, n * 512:(n + 1) * 512],
                                                                                  in0=xres[xs][:, n * 512:(n + 1) * 512],
                                                                                  in1=C.ps[bank][:, :], op=ALU.add),
                         reads=[xresb[xs], C.psb[bank]], writes=[xresb[xs]])
                k.dma(k.sp, dst[r0:r0 + 128, :], xres[xs][:], xresb[xs], reads=[xresb[xs]])

        L(0)
        N(0)
        T(0)
        for t in range(NT):
            nxt = None
            if t + 1 < NT:
                L(t + 1)
                nxt = (lambda t=t: N(t + 1))
            U(t, nxt)
            if t + 1 < NT:
                T(t + 1)
            Dn(t)
        k.barrier()
    k.release_dsems()


def build(S, plan, n_layers=DEPTH):
    nc = bass.Bass("TRN2", target_bir_lowering=False)
    C = Ctx()
    C.S = S
    inp = lambda name, shape, dt=F32: nc.dram_tensor(name, shape, dt, kind="ExternalInput").ap()
    C.x = inp("x", [S, D])
    C.c = inp("c", [D])
    C.ada_w = inp("ada_w", [DEPTH, D, 9 * D])
    C.ada_b = inp("ada_b", [DEPTH, 9 * D])
    C.norm_g = inp("norm_g", [DEPTH, 3, D])
    C.ffn_w13 = inp("ffn_w13", [DEPTH, 2, D, 2 * DFF])
    C.ffn_w2 = inp("ffn_w2", [DEPTH, 2, DFF, D])
    C.swa_w_in = inp("swa_w_in", [2, D, 1536])
    C.swa_w_out = inp("swa_w_out", [2, D, D])
    C.swa_q_g = inp("swa_q_g", [2, HD])
    C.swa_k_g = inp("swa_k_g", [2, HD])
    C.swa_sink = inp("swa_sink", [2, NH])
    C.swa_bias = inp("swa_bias", [128, 4096])
    C.fox_w_in = inp("fox_w_in", [2, D, 3088])
    C.fox_w_out = inp("fox_w_out", [2, D, D])
    C.fox_b_f = inp("fox_b_f", [2, NH])
    C.fox_q_g = inp("fox_q_g", [2, HD])
    C.fox_k_g = inp("fox_k_g", [2, HD])
    C.k_ident = inp("k_ident", [128, 128])
    C.k_swamask = inp("k_swamask", [128, 4096])
    C.k_negtri = inp("k_negtri", [128, 128])
    C.out = nc.dram_tensor("out", [S, D], F32, kind="ExternalOutput").ap()
    C.modv = nc.dram_tensor("modv", [DEPTH, 9 * D], F32, kind="Internal").ap()
    C.qx = nc.dram_tensor("qx", [NH, 65, S], BF16, kind="Internal").ap()
    C.kx = nc.dram_tensor("kx", [NH, 65, S], BF16, kind="Internal").ap()
    C.vx = nc.dram_tensor("vx", [NH, S, HD], BF16, kind="Internal").ap()
    C.oTd = nc.dram_tensor("oTd", [D, S], BF16, kind="Internal").ap()

    with ExitStack() as es:
        k = K(nc, es)
        C.ps = [es.enter_context(nc.psum_tensor(f"ps{i}", [128, 512], F32)) for i in range(8)]
        C.psb = [k.buf(f"ps{i}") for i in range(8)]
        C.ident = sb(nc, es, "ident", [128, 128], BF16)
        C.identf = sb(nc, es, "identf", [128, 128], F32)
        C.eps_t = sb(nc, es, "eps_t", [128, 1], F32)
        C.one_t = sb(nc, es, "one_t", [128, 1], F32)
        C.constb = k.buf("const", dma=True)
        block = es.enter_context(nc.Block())
        k.dma(k.sp, C.identf[:], C.k_ident, C.constb, writes=[C.constb])
        k.op(k.dve, lambda e: e.tensor_copy(out=C.ident[:], in_=C.identf[:]), reads=[C.constb], writes=[C.constb])
        k.op(k.dve, lambda e: e.memset(C.eps_t[:], EPS), writes=[C.constb])
        k.op(k.dve, lambda e: e.memset(C.one_t[:], 1.0), writes=[C.constb])
        keep = k.dnext
        k_release = k.release_dsems

        def release():
            k.dnext = keep
        k.release_dsems = release

        layers = sorted(set(p[1] for p in plan))
        phase_mods(k, C, layers)
        cur = C.x
        for (kind, l, i) in plan:
            if kind == "ffn":
                phase_ffn(k, C, l, i, cur, C.out)
            elif kind == "swa":
                phase_swa(k, C, l, cur, C.out)
            elif kind == "fox":
                phase_fox(k, C, l, cur, C.out)
            cur = C.out
        k.barrier()
    return nc


def emit_rstd_lnexp(k, C, ss, ssb, rs, rsb, n):
    k.op(k.act, lambda e: e.activation(out=rs[:], in_=ss[:], func=AF.Ln, bias=C.eps_t[:, 0:1], scale=1.0 / n),
         reads=[ssb, C.constb], writes=[rsb])
    k.op(k.act, lambda e: e.activation(out=rs[:], in_=rs[:], func=AF.Exp, scale=-0.5), reads=[rsb], writes=[rsb])


def emit_x_norm(k, C, xt, xb, p, ss, ssb, rs, rsb, hbf, hbfb, tmp, tmpb, a_t, a_b, sh_t, sh_b, hT, hTb, use_sqrt=False):
    k.op(k.dve, lambda e: e.memset(ss[p][:], 0.0), writes=[ssb[p]])
    for s in range(4):
        emit_norm_stats(k, C, xt[s][:], xb[s], ss[p], ssb[p], s, hbf[s % 2], hbfb[s % 2])
    emit_rstd_lnexp(k, C, ss[p], ssb[p], rs[p], rsb[p], D)
    for s in range(4):
        r = s % 2
        k.op(k.dve, lambda e, s=s, r=r: e.scalar_tensor_tensor(out=tmp[r][:], in0=xt[s][:], scalar=rs[p][:, s:s + 1], in1=a_t[:],
                                                                 op0=ALU.mult, op1=ALU.mult),
             reads=[xb[s], rsb[p], a_b], writes=[tmpb[r]])
        k.op(k.dve, lambda e, r=r: e.tensor_tensor(out=hbf[r][:], in0=tmp[r][:], in1=sh_t[:], op=ALU.add),
             reads=[tmpb[r], sh_b], writes=[hbfb[r]])
        emit_transposes(k, C, hbf[r], hbfb[r], hT, hTb[s], s, None)


def emit_qk_chunk(k, C, n, wt, wtb, col, hT, hTb, bones, sq, sqb, lnt, lntb, gcol, gb, dsts, dst_buf, pend, post=None):
    bank = C.nb()
    r = n % len(sq)

    def mm(e):
        for kc in range(KC):
            ins = e.matmul(C.ps[bank][:, :], lhsT=wt[:, kc, col:col + 128], rhs=hT[:, kc, :], start=(kc == 0), stop=(kc == KC - 1))
        return ins
    k.op(k.pe, mm, reads=[wtb] + hTb, writes=[C.psb[bank]])
    k.op(k.act, lambda e: e.activation(out=sq[r][:], in_=C.ps[bank][:, :], func=AF.Square), reads=[C.psb[bank]], writes=[sqb[r]])
    while len(pend) >= QK_DEPTH:
        pend.pop(0)()

    def tail():
        ssbank = C.nb()
        k.op(k.pe, lambda e: e.matmul(C.ps[ssbank][:, :], lhsT=bones[:], rhs=sq[r][:], start=True, stop=True),
             reads=[sqb[r], C.constb], writes=[C.psb[ssbank]])
        k.op(k.act, lambda e: e.activation(out=lnt[r][:], in_=C.ps[ssbank][:, :], func=AF.Ln, bias=C.eps_t[:, 0:1], scale=1.0 / HD),
             reads=[C.psb[ssbank], C.constb], writes=[lntb[r]])
        k.op(k.act, lambda e: e.activation(out=lnt[r][:], in_=lnt[r][:], func=AF.Exp, scale=-0.5), reads=[lntb[r]], writes=[lntb[r]])
        for (dst_ap, sl) in dsts:
            k.op(k.dve, lambda e, dst_ap=dst_ap, sl=sl: e.scalar_tensor_tensor(out=dst_ap, in0=C.ps[bank][sl, :], scalar=gcol[sl, 0:1],
                                                                               in1=lnt[r][sl, :], op0=ALU.mult, op1=ALU.mult),
                 reads=[C.psb[bank], lntb[r], gb], writes=[dst_buf])
        C.fb(ssbank)
        C.fb(bank)
        if post is not None:
            post()
    pend.append(tail)


def load_gcol(k, C, es, name, src_1d, scale):
    nc = k.nc
    t = sb(nc, es, name, [128, 1], F32)
    b = k.buf(name, dma=True)
    v = src_1d.rearrange("(p o) -> p o", o=1)
    k.dma(k.sp, t[0:64, :], v, b, writes=[b])
    k.dma(k.sp, t[64:128, :], v, b, writes=[b])
    if scale != 1.0:
        k.op(k.dve, lambda e: e.tensor_scalar(out=t[:], in0=t[:], scalar1=float(scale), scalar2=None, op0=ALU.mult), reads=[b], writes=[b])
    return t, b


def make_bones(k, C, es):
    nc = k.nc
    bones = sb(nc, es, "bones", [128, 128], BF16)
    k.op(k.dve, lambda e: e.memset(bones[:], 0.0), writes=[C.constb])
    k.op(k.dve, lambda e: e.memset(bones[0:64, 0:64], 1.0), writes=[C.constb])
    k.op(k.dve, lambda e: e.memset(bones[64:128, 64:128], 1.0), writes=[C.constb])
    return bones


def phase_swa(k, C, l, src, dst):
    nc = k.nc
    S = C.S
    NT = S // 512
    jl = l // 2
    with ExitStack() as es:
        xs = [[sb(nc, es, f"xs{p}_{s}", [128, D], F32) for s in range(4)] for p in range(2)]
        xsb = [[k.buf(f"xs{p}_{s}", dma=True) for s in range(4)] for p in range(2)]
        (a_t, a_b), (sh_t, sh_b), (ga_t, ga_b) = mod_prologue(k, C, es, l, 1, 1.0, ga=(xs[1][0], xsb[1][0]))
        w_in = C.swa_w_in
        wq = sb(nc, es, "wq", [128, KC, 1024], BF16)
        wqb = k.buf("wq", dma=True)
        for kc in range(KC):
            k.dma(k.pool, wq[:, kc, :], w_in[jl, kc * 128:(kc + 1) * 128, 0:1024], wqb, writes=[wqb])
        wkd = sb(nc, es, "wkd", [128, KC, 512], BF16)
        wkdb = k.buf("wkd", dma=True)
        for j in range(4):
            for hf in range(2):
                k.dma(k.pool, wkd[:, :, j * 128 + hf * 64: j * 128 + hf * 64 + 64],
                      w_in[jl, :, 1024 + j * 64:1024 + (j + 1) * 64].rearrange("(kc p) n -> p kc n", p=128), wkdb, writes=[wkdb])
        wv = sb(nc, es, "wv", [128, KC, 256], BF16)
        wvb = k.buf("wv", dma=True)
        k.dma(k.pool, wv[:], w_in[jl, :, 1280:1536].rearrange("(kc p) n -> p kc n", p=128), wvb, writes=[wvb])
        wo = sb(nc, es, "wo", [128, 8, D], BF16)
        wob = k.buf("wo")
        stage = [xs[1][1], xs[1][2]]
        stage_b = [xsb[1][1], xsb[1][2]]
        cnt = 0
        for jp in range(2):
            for g in range(4):
                for hf in range(2):
                    head = 4 * (2 * jp + hf) + g
                    load_scaled_weight(k, C, stage, stage_b, cnt, wo[hf * 64:(hf + 1) * 64, jp * 4 + g, :], wob,
                                       C.swa_w_out[jl, head * 64:(head + 1) * 64, :], ga_t, ga_b, rows=64, p0=hf * 64)
                    cnt += 1
        gq, gqb = load_gcol(k, C, es, "gq", C.swa_q_g[jl, :], HD ** -0.5)
        gk, gkb = load_gcol(k, C, es, "gk", C.swa_k_g[jl, :], 1.0)
        bones = make_bones(k, C, es)
        tmp = [sb(nc, es, f"tmp{j}", [128, D], F32) for j in range(2)]
        tmpb = [k.buf(f"tmp{j}", dma=True) for j in range(2)]
        tmp_pro, tmp_prob = tmp, tmpb
        BM = sb(nc, es, "BM", [128, 4096], BF16)
        BMb = k.buf("BM")
        for s in range(4):
            pass
        for s in range(4):
            bt_, bb_ = xs[0][s], xsb[0][s]
            mt_, mb_ = tmp_pro[s % 2], tmp_prob[s % 2]
            k.dma(k.sp, bt_[:], C.swa_bias[:, s * 1024:(s + 1) * 1024], bb_, writes=[bb_])
            k.dma(k.sp, mt_[:], C.k_swamask[:, s * 1024:(s + 1) * 1024], mb_, writes=[mb_])
            k.op(k.dve, lambda e, bt_=bt_, mt_=mt_: e.tensor_tensor(out=bt_[:], in0=bt_[:], in1=mt_[:], op=ALU.mult),
                 reads=[bb_, mb_], writes=[bb_])
            k.op(k.dve, lambda e, mt_=mt_: e.tensor_scalar(out=mt_[:], in0=mt_[:], scalar1=-NEG, scalar2=NEG, op0=ALU.mult, op1=ALU.add),
                 reads=[mb_], writes=[mb_])
            k.op(k.dve, lambda e, s=s, bt_=bt_, mt_=mt_: e.tensor_tensor(out=BM[:, s * 1024:(s + 1) * 1024], in0=bt_[:], in1=mt_[:], op=ALU.add),
                 reads=[bb_, mb_], writes=[BMb])
        s16 = sb(nc, es, "s16", [1, NH], F32)
        ESr = sb(nc, es, "ESr", [1, NH, 128], BF16)
        sel = sb(nc, es, "sel", [1, 2, 128], BF16)
        ESb = k.buf("ES", dma=True)
        k.dma(k.sp, s16[:], C.swa_sink[jl:jl + 1, :], ESb, writes=[ESb])
        k.op(k.act, lambda e: e.activation(out=s16[:], in_=s16[:], func=AF.Exp), reads=[ESb], writes=[ESb])
        k.op(k.dve, lambda e: e.tensor_copy(out=ESr[:], in_=s16[:].unsqueeze(2).to_broadcast([1, NH, 128])), reads=[ESb], writes=[ESb])
        k.op(k.dve, lambda e: e.memset(sel[:], 0.0), writes=[ESb])
        k.op(k.dve, lambda e: e.memset(sel[0:1, 0, 64:128], 1.0), writes=[ESb])
        k.op(k.dve, lambda e: e.memset(sel[0:1, 1, 0:64], 1.0), writes=[ESb])

        vext = sb(nc, es, "vext", [128, 8, 4, 128], BF16)
        vextb = [k.buf(f"vext{i}") for i in range(8)]
        k.op(k.dve, lambda e: e.memset(vext[:], 1.0), writes=vextb)
        kTa = sb(nc, es, "kTa", [128, 4, 1024], BF16)
        kTb = sb(nc, es, "kTb", [128, 4, 1024], BF16)
        k.op(k.dve, lambda e: e.memset(kTa[:], 0.0), writes=[C.constb])
        k.op(k.dve, lambda e: e.memset(kTb[:], 0.0), writes=[C.constb])
        kTdb = [[k.buf(f"kTd{j}_{h}") for h in range(2)] for j in range(4)]
        qT2 = [sb(nc, es, f"qT{p}", [128, 8, 512], BF16) for p in range(2)]
        qTb2 = [[k.buf(f"qT{p}_{c}") for c in range(8)] for p in range(2)]
        hbf = [sb(nc, es, f"hbf{j}", [128, D], BF16) for j in range(2)]
        hbfb = [k.buf(f"hbf{j}") for j in range(2)]
        hT = sb(nc, es, "hT", [128, KC, 512], BF16)
        hTb = [k.buf(f"hT{j}") for j in range(4)]
        sq = [sb(nc, es, f"sq{j}", [128, 512], BF16) for j in range(3)]
        sqb = [k.buf(f"sq{j}") for j in range(3)]
        lnt = [sb(nc, es, f"lnt{j}", [128, 512], F32) for j in range(3)]
        lntb = [k.buf(f"lnt{j}") for j in range(3)]
        NPT = 6
        pt = [sb(nc, es, f"pt{j}", [128, 512], BF16) for j in range(NPT)]
        ptb = [k.buf(f"pt{j}") for j in range(NPT)]
        ptc = [0]
        rd = [sb(nc, es, f"rd{j}", [128, 512], F32) for j in range(2)]
        rdb = [k.buf(f"rd{j}") for j in range(2)]
        oT = [sb(nc, es, f"oT{j}", [128, 8, 128], BF16) for j in range(2)]
        oTb = [k.buf(f"oT{j}") for j in range(2)]
        ss = [sb(nc, es, f"ss{j}", [128, 4], F32) for j in range(2)]
        ssb = [k.buf(f"ss{j}") for j in range(2)]
        rs = [sb(nc, es, f"rs{j}", [128, 4], F32) for j in range(2)]
        rsb = [k.buf(f"rs{j}") for j in range(2)]

        def L(t):
            for s in range(4):
                r0 = t * 512 + s * 128
                k.dma(k.sp, xs[t % 2][s][:], src[r0:r0 + 128, :], xsb[t % 2][s], writes=[xsb[t % 2][s]])

        def sT_unit(t, s, j, u):
            b = 4 * t + s
            qT, qTb = qT2[t % 2], qTb2[t % 2]
            kbs0 = [(1, b % 8)] if b == 0 else [(0, (b - 1) % 8), (1, b % 8)]
            kbs = []
            for (kbi, kslot) in kbs0:
                bank = C.nb()
                r = ptc[0] % NPT
                ptc[0] += 1
                kbs.append((kbi, kslot, bank, r))

                def mm(e, kslot=kslot, bank=bank, kbi=kbi):
                    off = (j * 2 + kbi) * 512
                    e.matmul(C.ps[bank][:, :], lhsT=C.ident[:], rhs=BM[:, off:off + 512], start=True, stop=False)
                    for g in range(4):
                        head = 4 * j + g
                        c, hf = head // 2, head % 2
                        kT_ = kTa if hf == 0 else kTb
                        ins = e.matmul(C.ps[bank][:, g * 128:(g + 1) * 128],
                                       lhsT=kT_[:, j, kslot * 128:(kslot + 1) * 128],
                                       rhs=qT[:, c, s * 128:(s + 1) * 128], start=False, stop=(g == 3))
                    return ins
                k.op(k.pe, mm, reads=[kTdb[j][kslot // 4], qTb[2 * j], qTb[2 * j + 1], C.constb, BMb], writes=[C.psb[bank]])
            return kbs

        def rest_unit(t, s, j, u, kbs):
            b = 4 * t + s
            jp = j // 2
            accb = C.nb()
            num = slice(0, 64) if j % 2 == 0 else slice(64, 128)
            den = slice(64, 128) if j % 2 == 0 else slice(0, 64)
            for (kbi, kslot, bank, r) in kbs:
                k.op(k.act, lambda e, r=r, bank=bank: e.activation(out=pt[r][:], in_=C.ps[bank][:, :], func=AF.Exp),
                     reads=[C.psb[bank]], writes=[ptb[r]])
                C.fb(bank)

            def pv(e):
                for idx, (kbi, kslot, bank, r) in enumerate(kbs):
                    e.matmul(C.ps[accb][:, :], lhsT=vext[:, kslot, j, :], rhs=pt[r][:], start=(idx == 0), stop=False)
                return e.matmul(C.ps[accb][:, :], lhsT=sel[0:1, j % 2, :], rhs=ESr[0:1, j * 4:(j + 1) * 4, :].rearrange("p g q -> p (g q)"),
                                start=False, stop=True)
            k.op(k.pe, pv, reads=[ptb[r] for (_, _, _, r) in kbs] + [vextb[ks] for (_, ks, _, _) in kbs] + [ESb], writes=[C.psb[accb]])
            r2 = u % 2
            acc3 = C.ps[accb][:, :].rearrange("p (g q) -> p g q", g=4)
            rd3 = rd[r2][:].rearrange("p (g q) -> p g q", g=4)
            k.op(k.dve, lambda e: e.reciprocal(out=rd3[den], in_=acc3[den]), reads=[C.psb[accb]], writes=[rdb[r2]])
            k.op(k.dve, lambda e: e.tensor_tensor(out=oT[b % 2][num, jp * 4:(jp + 1) * 4, :], in0=acc3[num], in1=rd3[den], op=ALU.mult),
                 reads=[C.psb[accb], rdb[r2]], writes=[oTb[b % 2]])
            C.fb(accb)

        def outproj(t, s):
            b = 4 * t + s
            r0 = t * 512 + s * 128
            xt, xb = xs[t % 2][s], xsb[t % 2][s]
            for n in range(2):
                bank = C.nb()

                def mm(e, n=n, bank=bank):
                    for ci in range(8):
                        ins = e.matmul(C.ps[bank][:, :], lhsT=oT[b % 2][:, ci, :], rhs=wo[:, ci, n * 512:(n + 1) * 512],
                                       start=(ci == 0), stop=(ci == 7))
                    return ins
                k.op(k.pe, mm, reads=[oTb[b % 2], wob], writes=[C.psb[bank]])
                k.op(k.dve, lambda e, n=n, bank=bank: e.tensor_tensor(out=xt[:, n * 512:(n + 1) * 512], in0=xt[:, n * 512:(n + 1) * 512],
                                                                      in1=C.ps[bank][:, :], op=ALU.add),
                     reads=[xb, C.psb[bank]], writes=[xb])
                C.fb(bank)
            k.dma(k.sp, dst[r0:r0 + 128, :], xt[:], xb, reads=[xb])

        def front(t):
            p = t % 2
            xt, xb = xs[p], xsb[p]
            qT, qTb = qT2[p], qTb2[p]
            k.op(k.dve, lambda e: e.memset(ss[p][:], 0.0), writes=[ssb[p]])
            for s in range(4):
                emit_norm_stats(k, C, xt[s][:], xb[s], ss[p], ssb[p], s, hbf[s % 2], hbfb[s % 2])
            emit_rstd_lnexp(k, C, ss[p], ssb[p], rs[p], rsb[p], D)
            yield
            for s in range(4):
                r = s % 2
                k.op(k.dve, lambda e, s=s, r=r: e.scalar_tensor_tensor(out=tmp[r][:], in0=xt[s][:], scalar=rs[p][:, s:s + 1], in1=a_t[:],
                                                                         op0=ALU.mult, op1=ALU.mult),
                     reads=[xb[s], rsb[p], a_b], writes=[tmpb[r]])
                k.op(k.dve, lambda e, r=r: e.tensor_tensor(out=hbf[r][:], in0=tmp[r][:], in1=sh_t[:], op=ALU.add),
                     reads=[tmpb[r], sh_b], writes=[hbfb[r]])
                emit_transposes(k, C, hbf[r], hbfb[r], hT, hTb[s], s, None)
                yield
            pend = []
            n = 0
            for c in range(8):
                emit_qk_chunk(k, C, n, wq, wqb, c * 128, hT, hTb, bones, sq, sqb, lnt, lntb, gq, gqb, [(qT[:, c, :], slice(0, 128))], qTb[c], pend)
                n += 1
                yield
            for j in range(4):
                c0 = p * 512
                emit_qk_chunk(k, C, n, wkd, wkdb, j * 128, hT, hTb, bones, sq, sqb, lnt, lntb, gk, gkb,
                              [(kTa[0:64, j, c0:c0 + 512], slice(0, 64)), (kTb[64:128, j, c0:c0 + 512], slice(64, 128))],
                              kTdb[j][p], pend)
                n += 1
                yield
            for s in range(4):
                bank = C.nb()
                rb = (4 * t + s) % 8

                def mm(e, s=s, bank=bank):
                    for kc in range(KC):
                        ins = e.matmul(C.ps[bank][:, 0:256], lhsT=hT[:, kc, s * 128:(s + 1) * 128], rhs=wv[:, kc, :],
                                       start=(kc == 0), stop=(kc == KC - 1))
                    return ins
                k.op(k.pe, mm, reads=[wvb, hTb[s]], writes=[C.psb[bank]])
                if pend:
                    pend.pop(0)()
                for j in range(4):
                    c0 = 0 if j % 2 == 0 else 64
                    k.op(k.act, lambda e, j=j, c0=c0, rb=rb, bank=bank: e.copy(out=vext[:, rb, j, c0:c0 + 64], in_=C.ps[bank][:, j * 64:(j + 1) * 64]),
                         reads=[C.psb[bank]], writes=[vextb[rb]])
                C.fb(bank)
                yield
            while pend:
                pend.pop(0)()

        def drain(g):
            for _ in g:
                pass

        L(0)
        drain(front(0))
        ucount = 0
        for t in range(NT):
            nxt = None
            if t + 1 < NT:
                L(t + 1)
                nxt = front(t + 1)
            units = [(s, j) for s in range(4) for j in range(4)]
            kb_next = sT_unit(t, units[0][0], units[0][1], ucount)
            nsteps = [int(os.environ.get("SWA_NSTEPS", "99"))]

            def step():
                if nxt is not None and nsteps[0] > 0:
                    nsteps[0] -= 1
                    next(nxt, None)
            for ui, (s, j) in enumerate(units):
                u = ucount
                kbs = kb_next
                if ui + 1 < len(units):
                    kb_next = sT_unit(t, units[ui + 1][0], units[ui + 1][1], u + 1)
                rest_unit(t, s, j, u, kbs)
                ucount += 1
                step()
                if j == 3:
                    outproj(t, s)
                    step()
            if nxt is not None:
                drain(nxt)
        k.barrier()
    k.release_dsems()


def phase_fox(k, C, l, src, dst):
    nc = k.nc
    S = C.S
    NB = S // 128
    with ExitStack() as esl:
        negF = sb(nc, esl, "negF", [128, NB, NH], F32)
        negFb = k.buf("negF")
        fox_A(k, C, l, src, negF, negFb)
        fox_B(k, C, l, negF, negFb)
        fox_C(k, C, l, src, dst)


def fox_A(k, C, l, src, negF, negFb):
    nc = k.nc
    S = C.S
    NT = S // 512
    jl = l // 2
    with ExitStack() as es:
        xs = [[sb(nc, es, f"xs{p}_{s}", [128, D], F32) for s in range(4)] for p in range(2)]
        xsb = [[k.buf(f"xs{p}_{s}", dma=True) for s in range(4)] for p in range(2)]
        (a_t, a_b), (sh_t, sh_b), _ = mod_prologue(k, C, es, l, 1, 1.0, ga=(xs[1][0], xsb[1][0]))
        w_in = C.fox_w_in
        wts = []
        for nm, c0 in (("wq", 0), ("wk", 1024), ("wv", 2048)):
            wt = sb(nc, es, nm, [128, KC, 1024], BF16)
            wb = k.buf(nm, dma=True)
            for kc in range(KC):
                k.dma(k.pool, wt[:, kc, :], w_in[jl, kc * 128:(kc + 1) * 128, c0:c0 + 1024], wb, writes=[wb])
            wts.append((wt, wb))
        (wq, wqb), (wk, wkb), (wv, wvb) = wts
        wf = sb(nc, es, "wf", [128, KC, NH], BF16)
        wfb = k.buf("wf", dma=True)
        k.dma(k.pool, wf[:], w_in[jl, :, 3072:3088].rearrange("(kc p) n -> p kc n", p=128), wfb, writes=[wfb])
        gq, gqb = load_gcol(k, C, es, "gq", C.fox_q_g[jl, :], HD ** -0.5)
        gk, gkb = load_gcol(k, C, es, "gk", C.fox_k_g[jl, :], 1.0)
        bones = make_bones(k, C, es)
        negbf = sb(nc, es, "negbf", [NH, 1], F32)
        nbb = k.buf("negbf", dma=True)
        k.dma(k.sp, negbf[:], C.fox_b_f[jl, :].rearrange("(p o) -> p o", o=1), nbb, writes=[nbb])
        k.op(k.dve, lambda e: e.tensor_scalar(out=negbf[:], in0=negbf[:], scalar1=-1.0, scalar2=None, op0=ALU.mult), reads=[nbb], writes=[nbb])
        ones16 = sb(nc, es, "ones16", [NH, 512], F32)
        onesbf = sb(nc, es, "onesbf", [NH, 512], BF16)
        negI = sb(nc, es, "negI", [NH, NH], F32)
        cb2 = k.buf("fconst", dma=True)
        k.op(k.dve, lambda e: e.memset(ones16[:], 1.0), writes=[cb2])
        k.op(k.dve, lambda e: e.memset(onesbf[:], 1.0), writes=[cb2])
        k.op(k.dve, lambda e: e.tensor_scalar(out=negI[:], in0=C.identf[0:NH, 0:NH], scalar1=-1.0, scalar2=None, op0=ALU.mult),
             reads=[C.constb], writes=[cb2])

        hbf = [sb(nc, es, f"hbf{j}", [128, D], BF16) for j in range(2)]
        hbfb = [k.buf(f"hbf{j}") for j in range(2)]
        hT = sb(nc, es, "hT", [128, KC, 512], BF16)
        hTb = [k.buf(f"hT{j}") for j in range(4)]
        tmp = [sb(nc, es, f"tmp{j}", [128, D], F32) for j in range(2)]
        tmpb = [k.buf(f"tmp{j}") for j in range(2)]
        sq = [sb(nc, es, f"sq{j}", [128, 512], BF16) for j in range(3)]
        sqb = [k.buf(f"sq{j}") for j in range(3)]
        lnt = [sb(nc, es, f"lnt{j}", [128, 512], F32) for j in range(3)]
        lntb = [k.buf(f"lnt{j}") for j in range(3)]
        NQ = 4
        qn = [sb(nc, es, f"qn{j}", [128, 512], BF16) for j in range(NQ)]
        qnb = [k.buf(f"qn{j}", dma=True) for j in range(NQ)]
        vt = [sb(nc, es, f"vt{j}", [128, 512], BF16) for j in range(2)]
        vtb = [k.buf(f"vt{j}", dma=True) for j in range(2)]
        lf = [sb(nc, es, f"lf{j}", [NH, 512], F32) for j in range(2)]
        lfb = [k.buf(f"lf{j}") for j in range(2)]
        FT = [sb(nc, es, f"FT{j}", [NH, 512], F32) for j in range(2)]
        FTb = [k.buf(f"FT{j}") for j in range(2)]
        Rb = [sb(nc, es, f"Rb{j}", [NH, 512], BF16) for j in range(2)]
        Rbb = [k.buf(f"Rb{j}", dma=True) for j in range(2)]
        ss = [sb(nc, es, f"ss{j}", [128, 4], F32) for j in range(2)]
        ssb = [k.buf(f"ss{j}") for j in range(2)]
        rs = [sb(nc, es, f"rs{j}", [128, 4], F32) for j in range(2)]
        rsb = [k.buf(f"rs{j}") for j in range(2)]

        def L(t):
            for s in range(4):
                r0 = t * 512 + s * 128
                k.dma(k.sp, xs[t % 2][s][:], src[r0:r0 + 128, :], xsb[t % 2][s], writes=[xsb[t % 2][s]])

        L(0)
        n = 0
        qi = 0
        for t in range(NT):
            if t + 1 < NT:
                L(t + 1)
            c0 = t * 512
            emit_x_norm(k, C, xs[t % 2], xsb[t % 2], t % 2, ss, ssb, rs, rsb, hbf, hbfb, tmp, tmpb, a_t, a_b, sh_t, sh_b, hT, hTb)
            pend = []
            for (wt, wb, gcol, gb, dram) in ((wq, wqb, gq, gqb, C.qx), (wk, wkb, gk, gkb, C.kx)):
                for c in range(8):
                    r = qi % NQ
                    qi += 1

                    def post(r=r, c=c, dram=dram):
                        for hf in range(2):
                            k.dma(k.sp, dram[2 * c + hf, 0:64, c0:c0 + 512], qn[r][hf * 64:(hf + 1) * 64, :], qnb[r], reads=[qnb[r]])
                    emit_qk_chunk(k, C, n, wt, wb, c * 128, hT, hTb, bones, sq, sqb, lnt, lntb, gcol, gb,
                                  [(qn[r][:], slice(0, 128))], qnb[r], pend, post=post)
                    n += 1
            fb = C.nb()

            def mmf(e):
                for kc in range(KC):
                    ins = e.matmul(C.ps[fb][0:NH, :], lhsT=wf[:, kc, :], rhs=hT[:, kc, :], start=(kc == 0), stop=(kc == KC - 1))
                return ins
            k.op(k.pe, mmf, reads=[wfb] + hTb, writes=[C.psb[fb]])
            pend.pop(0)()
            p = t % 2
            k.op(k.act, lambda e, p=p: e.activation(out=lf[p][:], in_=C.ps[fb][0:NH, :], func=AF.Exp, bias=negbf[:, 0:1], scale=-1.0),
                 reads=[C.psb[fb], nbb], writes=[lfb[p]])
            k.op(k.act, lambda e, p=p: e.activation(out=lf[p][:], in_=lf[p][:], func=AF.Ln, bias=C.one_t[0:NH, 0:1], scale=1.0),
                 reads=[lfb[p], C.constb], writes=[lfb[p]])
            init = 0.0 if t == 0 else FT[1 - p][:, 511:512]
            k.op(k.dve, lambda e, p=p, init=init: e.tensor_tensor_scan(out=FT[p][:], data0=ones16[:], data1=lf[p][:], initial=init,
                                                                      op0=ALU.mult, op1=ALU.subtract),
                 reads=[lfb[p], cb2, FTb[1 - p]], writes=[FTb[p]])
            k.op(k.dve, lambda e, p=p: e.tensor_copy(out=Rb[p][:], in_=FT[p][:]), reads=[FTb[p]], writes=[Rbb[p]])
            k.dma(k.sp, C.qx[:, 64, c0:c0 + 512], Rb[p][:], Rbb[p], reads=[Rbb[p]])
            k.dma(k.sp, C.kx[:, 64, c0:c0 + 512], onesbf[:], cb2, reads=[cb2])
            for s in range(4):
                def mmt(e, s=s, p=p):
                    return e.matmul(C.ps[fb][:, 16 + s * 16:32 + s * 16], lhsT=FT[p][:, s * 128:(s + 1) * 128], rhs=negI[:, :], start=True, stop=True)
                k.op(k.pe, mmt, reads=[FTb[p], cb2], writes=[C.psb[fb]])
            k.op(k.act, lambda e: e.copy(out=negF[:, 4 * t:4 * t + 4, :], in_=C.ps[fb][:, 16:80].rearrange("p (s h) -> p s h", s=4)),
                 reads=[C.psb[fb]], writes=[negFb])
            C.fb(fb)
            for s in range(4):
                r0 = t * 512 + s * 128
                for n2 in range(2):
                    bank = C.nb()
                    n += 1
                    vr = (2 * s + n2) % 2

                    def mm(e, s=s, n2=n2, bank=bank):
                        for kc in range(KC):
                            ins = e.matmul(C.ps[bank][:, :], lhsT=hT[:, kc, s * 128:(s + 1) * 128], rhs=wv[:, kc, n2 * 512:(n2 + 1) * 512],
                                           start=(kc == 0), stop=(kc == KC - 1))
                        return ins
                    k.op(k.pe, mm, reads=[wvb, hTb[s]], writes=[C.psb[bank]])
                    if pend:
                        pend.pop(0)()
                    k.op(k.act, lambda e, vr=vr, bank=bank: e.copy(out=vt[vr][:], in_=C.ps[bank][:, :]), reads=[C.psb[bank]], writes=[vtb[vr]])
                    C.fb(bank)
                    k.dma(k.sp, C.vx[n2 * 8:(n2 + 1) * 8, r0:r0 + 128, :].rearrange("h t d -> t h d"),
                          vt[vr][:].rearrange("p (h d) -> p h d", h=8), vtb[vr], reads=[vtb[vr]])
            while pend:
                pend.pop(0)()
        k.barrier()
    k.release_dsems()


def fox_B(k, C, l, negF, negFb):
    nc = k.nc
    S = C.S
    NT = S // 512
    NB = S // 128
    with ExitStack() as es:
        qs = [sb(nc, es, f"qs{j}", [65, S], BF16) for j in range(2)]
        ks = [sb(nc, es, f"ks{j}", [65, S], BF16) for j in range(2)]
        vs = [sb(nc, es, f"vs{j}", [128, NB, 128], BF16) for j in range(2)]
        qsb = [k.buf(f"qs{j}", dma=True) for j in range(2)]
        ksb = [k.buf(f"ks{j}", dma=True) for j in range(2)]
        vsb = [k.buf(f"vs{j}", dma=True) for j in range(2)]
        k.op(k.dve, lambda e: e.memset(vs[0][:, :, 64:128], 1.0), writes=[vsb[0]])
        k.op(k.dve, lambda e: e.memset(vs[1][:, :, 0:64], 1.0), writes=[vsb[1]])
        negtri = sb(nc, es, "negtri", [128, 128], F32)
        ntb = k.buf("negtri", dma=True)
        k.dma(k.sp, negtri[:], C.k_negtri, ntb, writes=[ntb])
        NP = 3
        pT = [sb(nc, es, f"pT{j}", [128, 512], BF16) for j in range(NP)]
        pTb = [k.buf(f"pT{j}") for j in range(NP)]
        rdn = [sb(nc, es, f"rdn{j}", [128, 512], F32) for j in range(2)]
        rdnb = [k.buf(f"rdn{j}") for j in range(2)]
        oTs = [sb(nc, es, f"oTs{j}", [128, 512], BF16) for j in range(2)]
        oTsb = [k.buf(f"oTs{j}", dma=True) for j in range(2)]

        def load(h):
            hb = h % 2
            k.dma(k.sp, qs[hb][:], C.qx[h, :, :], qsb[hb], writes=[qsb[hb]])
            k.dma(k.sp, ks[hb][:], C.kx[h, :, :], ksb[hb], writes=[ksb[hb]])
            vc = 0 if hb == 0 else 64
            k.dma(k.sp, vs[hb][:, :, vc:vc + 64], C.vx[h, :, :].rearrange("(b p) d -> p b d", p=128), vsb[hb], writes=[vsb[hb]])

        load(0)
        tcount = 0
        ucount = 0
        for h in range(NH):
            hb = h % 2
            if h + 1 < NH:
                load(h + 1)
            num = slice(0, 64) if hb == 0 else slice(64, 128)
            den = slice(64, 128) if hb == 0 else slice(0, 64)
            units = [(t, kb) for t in range(NT) for kb in range(4 * t + 4)]

            def qk(ui):
                t, kb = units[ui]
                u = ucount + ui
                bank = u % NP
                c0 = 128 * (kb - 4 * t) if kb >= 4 * t else 0
                k.op(k.pe, lambda e: e.matmul(C.ps[bank][:, c0:512], lhsT=ks[hb][:, kb * 128:(kb + 1) * 128],
                                              rhs=qs[hb][:, t * 512 + c0:(t + 1) * 512], start=True, stop=True),
                     reads=[ksb[hb], qsb[hb]], writes=[C.psb[bank]])

            qk(0)
            if len(units) > 1:
                qk(1)
            for ui, (t, kb) in enumerate(units):
                u = ucount + ui
                bank = u % NP
                r = u % NP
                diag = kb >= 4 * t
                c0 = 128 * (kb - 4 * t) if diag else 0
                accb = 4 + (tcount + t) % 2
                if ui + 2 < len(units):
                    qk(ui + 2)
                if diag:
                    k.op(k.dve, lambda e: e.tensor_tensor(out=C.ps[bank][:, c0:c0 + 128], in0=C.ps[bank][:, c0:c0 + 128], in1=negtri[:], op=ALU.add),
                         reads=[C.psb[bank], ntb], writes=[C.psb[bank]])
                k.op(k.act, lambda e: e.activation(out=pT[r][:, c0:512], in_=C.ps[bank][:, c0:512], func=AF.Exp, bias=negF[:, kb, h:h + 1], scale=1.0),
                     reads=[C.psb[bank], negFb], writes=[pTb[r]])
                last = (kb == 4 * t + 3)
                k.op(k.pe, lambda e: e.matmul(C.ps[accb][:, c0:512], lhsT=vs[hb][:, kb, :], rhs=pT[r][:, c0:512], start=(kb == 0), stop=last),
                     reads=[pTb[r], vsb[hb]], writes=[C.psb[accb]])
                if last:
                    r2 = (tcount + t) % 2
                    k.op(k.dve, lambda e: e.reciprocal(out=rdn[r2][den, :], in_=C.ps[accb][den, :]), reads=[C.psb[accb]], writes=[rdnb[r2]])
                    k.op(k.dve, lambda e: e.tensor_tensor(out=oTs[r2][num, :], in0=C.ps[accb][num, :], in1=rdn[r2][den, :], op=ALU.mult),
                         reads=[C.psb[accb], rdnb[r2]], writes=[oTsb[r2]])
                    k.dma(k.sp, C.oTd[h * 64:(h + 1) * 64, t * 512:(t + 1) * 512], oTs[r2][num, :], oTsb[r2], reads=[oTsb[r2]])
            ucount += len(units)
            tcount += NT
        k.barrier()
    k.release_dsems()


def fox_C(k, C, l, src, dst):
    nc = k.nc
    S = C.S
    NT = S // 512
    jl = l // 2
    with ExitStack() as es:
        xs = [[sb(nc, es, f"xs{p}_{s}", [128, D], F32) for s in range(4)] for p in range(2)]
        xsb = [[k.buf(f"xs{p}_{s}", dma=True) for s in range(4)] for p in range(2)]
        _, _, (ga_t, ga_b) = mod_prologue(k, C, es, l, 1, 1.0, ga=(xs[1][0], xsb[1][0]))
        wo = sb(nc, es, "wo", [128, 8, D], BF16)
        wob = k.buf("wo")
        stage = [xs[1][1], xs[1][2]]
        stage_b = [xsb[1][1], xsb[1][2]]
        for c in range(8):
            load_scaled_weight(k, C, stage, stage_b, c, wo[:, c, :], wob, C.fox_w_out[jl, c * 128:(c + 1) * 128, :], ga_t, ga_b)
        oTt = [sb(nc, es, f"oTt{j}", [128, 8, 512], BF16) for j in range(2)]
        oTtb = [k.buf(f"oTt{j}", dma=True) for j in range(2)]

        def L(t):
            k.dma(k.sp, oTt[t % 2][:], C.oTd[:, t * 512:(t + 1) * 512].rearrange("(c p) t -> p c t", p=128), oTtb[t % 2], writes=[oTtb[t % 2]])
            for s in range(4):
                r0 = t * 512 + s * 128
                k.dma(k.sp, xs[t % 2][s][:], src[r0:r0 + 128, :], xsb[t % 2][s], writes=[xsb[t % 2][s]])

        L(0)
        for t in range(NT):
            if t + 1 < NT:
                L(t + 1)
            for s in range(4):
                r0 = t * 512 + s * 128
                xt, xb = xs[t % 2][s], xsb[t % 2][s]
                for n in range(2):
                    bank = (2 * s + n) % 4

                    def mm(e, s=s, n=n, bank=bank):
                        for c in range(8):
                            ins = e.matmul(C.ps[bank][:, :], lhsT=oTt[t % 2][:, c, s * 128:(s + 1) * 128], rhs=wo[:, c, n * 512:(n + 1) * 512],
                                           start=(c == 0), stop=(c == 7))
                        return ins
                    k.op(k.pe, mm, reads=[oTtb[t % 2], wob], writes=[C.psb[bank]])
                    k.op(k.dve, lambda e, xt=xt, n=n, bank=bank: e.tensor_tensor(out=xt[:, n * 512:(n + 1) * 512], in0=xt[:, n * 512:(n + 1) * 512],
                                                                                in1=C.ps[bank][:, :], op=ALU.add),
                         reads=[xb, C.psb[bank]], writes=[xb])
                k.dma(k.sp, dst[r0:r0 + 128, :], xt[:], xb, reads=[xb])
        k.barrier()
    k.release_dsems()


def rel_bucket_np(dist):
    n = np.maximum(dist, 0)
    max_exact = 16
    nf = np.maximum(n, 1).astype(np.float32)
    large = max_exact + (np.log(nf / max_exact) / math.log(128 / max_exact) * (32 - max_exact)).astype(np.int32)
    large = np.minimum(large, 31)
    return np.where(n < max_exact, n, large)


def host_constants(rel_bias):
    kk = np.arange(128)[:, None]
    q = np.arange(128)[None, :]
    bias = np.zeros((128, 4, 2, 4, 128), np.float32)
    mask = np.zeros((128, 4, 2, 4, 128), np.float32)
    for kb in range(2):
        dist = q - kk + (128 if kb == 0 else 0)
        valid = (dist >= 0) & (dist < 128)
        idx = rel_bucket_np(dist)
        for j in range(4):
            for g in range(4):
                bias[:, j, kb, g, :] = rel_bias[idx, 4 * j + g]
                mask[:, j, kb, g, :] = valid
    negtri = np.where(kk <= q, 0.0, NEG).astype(np.float32)
    return dict(swa_bias=bias.reshape(128, 4096), k_swamask=mask.reshape(128, 4096),
                k_negtri=negtri, k_ident=np.eye(128, dtype=np.float32))


FULL_PLAN = []
for _l in range(DEPTH):
    FULL_PLAN += [("ffn", _l, 0), ("swa" if _l % 2 == 0 else "fox", _l, 1), ("ffn", _l, 2)]


def make_in_maps(inputs, n_cores):
    consts = host_constants(np.asarray(inputs["rel_bias"], np.float32))
    shared = {kname: np.ascontiguousarray(np.asarray(inputs[kname], np.float32)) for kname in
              ("ada_w", "ada_b", "norm_g", "ffn_w13", "ffn_w2", "swa_w_in", "swa_w_out", "swa_q_g", "swa_k_g",
               "swa_sink", "fox_w_in", "fox_w_out", "fox_b_f", "fox_q_g", "fox_k_g")}
    shared.update(consts)
    x = np.asarray(inputs["x"], np.float32)
    c = np.asarray(inputs["c"], np.float32)
    maps = []
    for b in range(n_cores):
        m = dict(shared)
        m["x"] = np.ascontiguousarray(x[b])
        m["c"] = np.ascontiguousarray(c[b])
        maps.append(m)
    return maps


def kernel(**inputs):
    x = np.asarray(inputs["x"])
    B, S, _ = x.shape
    nc = build(S, FULL_PLAN)
    maps = make_in_maps(inputs, B)
    res = run_bass_kernel_spmd(nc, maps, core_ids=list(range(B)))
    return np.stack([np.asarray(r["out"], np.float32) for r in res.results], axis=0)
```

```python
import math
import os
from contextlib import ExitStack

import numpy as np
import concourse.bass as bass
import concourse.mybir as mybir
from concourse.bass_utils import run_bass_kernel_spmd

F32 = mybir.dt.float32
BF16 = mybir.dt.bfloat16
AF = mybir.ActivationFunctionType
ALU = mybir.AluOpType

D = 1024
DFF = 2816
HD = 64
NH = 16
KC = D // 128
FC = DFF // 128
EPS = 1e-6
DEPTH = 4
NEG = -30000.0
QK_DEPTH = 2


class Eng:
    def __init__(self, name, h, sem):
        self.name, self.h, self.sem = name, h, sem
        self.count = 0
        self.waited = {}

    def wait(self, sem, val):
        k = id(sem)
        if self.waited.get(k, 0) >= val:
            return
        self.h.wait_ge(sem, val)
        self.waited[k] = val


class DS:
    def __init__(self, sem):
        self.sem = sem
        self.count = 0


class Buf:
    __slots__ = ("name", "w", "r", "ds")

    def __init__(self, name, ds=None):
        self.name = name
        self.w = {}
        self.r = {}
        self.ds = ds


class K:
    def __init__(self, nc, es, n_dsem=48):
        self.nc = nc
        mk = lambda n: es.enter_context(nc.semaphore(n))
        self.pe = Eng("pe", nc.tensor, mk("s_pe"))
        self.act = Eng("act", nc.scalar, mk("s_act"))
        self.dve = Eng("dve", nc.vector, mk("s_dve"))
        self.pool = Eng("pool", nc.gpsimd, mk("s_pool"))
        self.sp = Eng("sp", nc.sync, mk("s_sp"))
        self.bar_sem = mk("s_bar")
        self.bar_count = 0
        self.dpool = [DS(mk(f"s_d{i}")) for i in range(n_dsem)]
        self.dnext = 0
        self.engs = [self.pe, self.act, self.dve, self.pool]

    def buf(self, name, dma=False):
        ds = None
        if dma:
            ds = self.dpool[self.dnext]
            self.dnext += 1
        return Buf(name, ds)

    def release_dsems(self):
        self.dnext = 0

    def _deps(self, eng, reads, writes, skip_ds=None):
        for b in reads:
            for (sem, val, src) in b.w.values():
                eng.wait(sem, val)
        for b in writes:
            for (sem, val, src) in b.w.values():
                if src is eng:
                    continue
                if skip_ds is not None and sem is skip_ds.sem:
                    continue
                eng.wait(sem, val)
            for (sem, val, src) in b.r.values():
                if src is eng:
                    continue
                eng.wait(sem, val)

    def op(self, eng, fn, reads=(), writes=()):
        self._deps(eng, reads, writes)
        ins = fn(eng.h)
        eng.count += 1
        ins.then_inc(eng.sem, 1)
        tok = (eng.sem, eng.count, eng)
        k = id(eng.sem)
        for b in reads:
            b.r[k] = tok
        for b in writes:
            b.w = {k: tok}
            b.r = {}
        return ins

    def dma(self, q, out, in_, slot, reads=(), writes=(), **kw):
        ds = slot.ds
        self._deps(q, reads, writes, skip_ds=ds)
        ins = q.h.dma_start(out=out, in_=in_, **kw)
        ds.count += 16
        ins.then_inc(ds.sem, 16)
        tok = (ds.sem, ds.count, None)
        k = id(ds.sem)
        for b in reads:
            b.r[k] = tok
        for b in writes:
            if k in b.w:
                b.w[k] = tok
            else:
                b.w = {k: tok}
                b.r = {}
        return ins

    def barrier(self):
        sp = self.sp
        for e in self.engs:
            if e.count:
                sp.wait(e.sem, e.count)
        for ds in self.dpool:
            if ds.count:
                sp.wait(ds.sem, ds.count)
        self.bar_count += 1
        sp.h.sem_inc(self.bar_sem, 1)
        for e in self.engs:
            e.h.wait_ge(self.bar_sem, self.bar_count)


class Ctx:
    _nb = 0
    _busy = None

    def nb(self):
        if self._busy is None:
            self._busy = [False] * 8
        for i in range(8):
            b = (self._nb + i) % 8
            if not self._busy[b]:
                self._busy[b] = True
                self._nb = (b + 1) % 8
                return b
        raise RuntimeError("no free PSUM bank")

    def fb(self, b):
        self._busy[b] = False


_uid = [0]


def sb(nc, es, name, shape, dt):
    _uid[0] += 1
    return es.enter_context(nc.sbuf_tensor(f"{name}_{_uid[0]}", shape, dt))


def bcast_row(dram_ap_1d, n):
    return dram_ap_1d.partition_broadcast(128)


def load_bcast(k, C, es, name, src_1d, n=D):
    nc = k.nc
    t = sb(nc, es, name, [128, n], F32)
    b = k.buf(name, dma=True)
    k.dma(k.sp, t[:], bcast_row(src_1d, n), b, writes=[b])
    return t, b


def emit_norm_stats(k, C, xt, xb, ss, ssb, col, junk, junkb):
    k.op(k.act, lambda e: e.activation(out=junk[:], in_=xt, func=AF.Square, accum_out=ss[:, col:col + 1]),
         reads=[xb], writes=[ssb, junkb])


def phase_mods(k, C, layers):
    nc = k.nc
    with ExitStack() as es:
        c_sb = sb(nc, es, "c_sb", [128, KC], F32)
        cact = sb(nc, es, "cact", [128, KC], F32)
        cb = k.buf("c", dma=True)
        cab = k.buf("cact")
        k.dma(k.sp, c_sb[:], C.c.rearrange("(kc p) -> p kc", p=128), cb, writes=[cb],
              allow_slow_non_contiguous=True)
        k.op(k.act, lambda e: e.activation(out=cact[:], in_=c_sb[:], func=AF.Silu), reads=[cb], writes=[cab])
        NR = 3
        wt = [sb(nc, es, f"adaw{i}", [128, KC, 512], F32) for i in range(NR)]
        wb = [k.buf(f"adaw{i}", dma=True) for i in range(NR)]
        bt = [sb(nc, es, f"adab{i}", [1, 512], F32) for i in range(NR)]
        bb = [k.buf(f"adab{i}", dma=True) for i in range(NR)]
        ot = [sb(nc, es, f"modo{i}", [1, 512], F32) for i in range(NR)]
        ob = [k.buf(f"modo{i}", dma=True) for i in range(NR)]
        n = 0
        for l in layers:
            for ct in range(9 * D // 512):
                s = n % NR
                pb = n % 2
                k.dma(k.sp, wt[s][:], C.ada_w[l, :, ct * 512:(ct + 1) * 512].rearrange("(kc p) n -> p kc n", p=128),
                      wb[s], writes=[wb[s]])
                k.dma(k.sp, bt[s][:], C.ada_b[l:l + 1, ct * 512:(ct + 1) * 512], bb[s], writes=[bb[s]])

                def mm(e, s=s, pb=pb):
                    for kc in range(KC):
                        ins = e.matmul(C.ps[pb][0:1, :], lhsT=cact[:, kc:kc + 1], rhs=wt[s][:, kc, :],
                                       start=(kc == 0), stop=(kc == KC - 1))
                    return ins
                k.op(k.pe, mm, reads=[cab, wb[s]], writes=[C.psb[pb]])
                k.op(k.dve, lambda e, s=s, pb=pb: e.tensor_tensor(out=ot[s][:], in0=C.ps[pb][0:1, :], in1=bt[s][:], op=ALU.add),
                     reads=[C.psb[pb], bb[s]], writes=[ob[s]])
                k.dma(k.sp, C.modv[l:l + 1, ct * 512:(ct + 1) * 512], ot[s][:], ob[s], reads=[ob[s]])
                n += 1
        k.barrier()
    k.release_dsems()


def mod_prologue(k, C, es, l, i, gate_scale, ga=None):
    nc = k.nc
    a_t, a_b = load_bcast(k, C, es, "a_b", C.modv[l, (i * 3 + 1) * D:(i * 3 + 2) * D])
    sh_t, sh_b = load_bcast(k, C, es, "sh_b", C.norm_g[l, i, :])
    if ga is None:
        ga_t, ga_b = load_bcast(k, C, es, "ga_b", C.modv[l, (i * 3 + 2) * D:(i * 3 + 3) * D])
    else:
        ga_t, ga_b = ga
        k.dma(k.sp, ga_t[:], bcast_row(C.modv[l, (i * 3 + 2) * D:(i * 3 + 3) * D], D), ga_b, writes=[ga_b])
    k.op(k.dve, lambda e: e.scalar_tensor_tensor(out=a_t[:], in0=a_t[:], scalar=1.0, in1=sh_t[:], op0=ALU.add, op1=ALU.mult),
         reads=[a_b, sh_b], writes=[a_b])
    k.op(k.dve, lambda e: e.tensor_scalar(out=ga_t[:], in0=ga_t[:], scalar1=float(gate_scale), scalar2=None, op0=ALU.mult),
         reads=[ga_b], writes=[ga_b])
    k.dma(k.sp, sh_t[:], bcast_row(C.modv[l, (i * 3 + 0) * D:(i * 3 + 1) * D], D), sh_b, writes=[sh_b])
    return (a_t, a_b), (sh_t, sh_b), (ga_t, ga_b)


def load_scaled_weight(k, C, stage, stage_b, cnt, dst_ap, dst_buf, src_ap, ga_t, ga_b, rows=128, p0=0):
    s = cnt % len(stage)
    k.dma(k.sp, stage[s][p0:p0 + rows, :], src_ap, stage_b[s], writes=[stage_b[s]])
    k.op(k.dve, lambda e: e.tensor_tensor(out=dst_ap, in0=stage[s][p0:p0 + rows, :], in1=ga_t[p0:p0 + rows, :], op=ALU.mult),
         reads=[stage_b[s], ga_b], writes=[dst_buf])


def emit_transposes(k, C, h_t, h_b, hT, hT_b, s, tpi):
    auto = tpi is None
    if auto:
        tpi = C.nb()
    tpb = C.ps[tpi][:].bitcast(BF16)

    def tr(e):
        for kc in range(KC):
            ins = e.transpose(tpb[:, kc * 128:(kc + 1) * 128], h_t[:, kc * 128:(kc + 1) * 128], C.ident[:])
        return ins
    k.op(k.pe, tr, reads=[h_b, C.constb], writes=[C.psb[tpi]])
    k.op(k.act, lambda e: e.copy(out=hT[:, :, s * 128:(s + 1) * 128], in_=tpb.rearrange("p (k t) -> p k t", k=KC)),
         reads=[C.psb[tpi]], writes=[hT_b])
    if auto:
        C.fb(tpi)


def phase_ffn(k, C, l, i, src, dst):
    nc = k.nc
    S = C.S
    NT = S // 512
    fi = i // 2
    with ExitStack() as es:
        xin = [sb(nc, es, f"xin{j}", [128, D], F32) for j in range(4)]
        xinb = [k.buf(f"xin{j}", dma=True) for j in range(4)]
        xres = [sb(nc, es, f"xres{j}", [128, D], F32) for j in range(2)]
        xresb = [k.buf(f"xres{j}", dma=True) for j in range(2)]
        (a_t, a_b), (sh_t, sh_b), (ga_t, ga_b) = mod_prologue(k, C, es, l, i, 0.5, ga=(xin[0], xinb[0]))
        w13 = sb(nc, es, "w13", [128, KC, 2 * DFF], BF16)
        w2 = sb(nc, es, "w2", [128, FC, D], BF16)
        GB = [0, 6, 12, 17, 22]
        NG = 4
        gof = [g for g in range(NG) for _ in range(GB[g + 1] - GB[g])]
        w2b = [k.buf(f"w2_{j}") for j in range(FC)]
        w13g = [k.buf(f"w13g{g}", dma=True) for g in range(NG)]
        for g in range(NG):
            for half in range(2):
                c0 = half * DFF + GB[g] * 128
                c1 = half * DFF + GB[g + 1] * 128
                for kc in range(KC):
                    k.dma(k.pool, w13[:, kc, c0:c1], C.ffn_w13[l, fi, kc * 128:(kc + 1) * 128, c0:c1], w13g[g], writes=[w13g[g]])
        stage = xres
        stage_b = xresb
        for j in range(FC):
            load_scaled_weight(k, C, stage, stage_b, j, w2[:, j, :], w2b[j], C.ffn_w2[l, fi, j * 128:(j + 1) * 128, :], ga_t, ga_b)

        hbf = [sb(nc, es, f"hbf{j}", [128, D], BF16) for j in range(4)]
        hbfb = [k.buf(f"hbf{j}") for j in range(4)]
        hT = sb(nc, es, "hT", [128, KC, 512], BF16)
        hTb = [k.buf(f"hT{j}") for j in range(4)]
        actT = sb(nc, es, "actT", [128, FC, 512], BF16)
        actb = [k.buf(f"act{j}") for j in range(FC)]
        sg = [sb(nc, es, f"sg{j}", [128, 512], F32) for j in range(2)]
        sgb = [k.buf(f"sg{j}") for j in range(2)]
        ss = [sb(nc, es, f"ss{j}", [128, 4], F32) for j in range(2)]
        ssb = [k.buf(f"ss{j}") for j in range(2)]
        rs = [sb(nc, es, f"rs{j}", [128, 4], F32) for j in range(2)]
        rsb = [k.buf(f"rs{j}") for j in range(2)]

        def L(t):
            for s in range(4):
                r0 = t * 512 + s * 128
                k.dma(k.sp, xin[s][:], src[r0:r0 + 128, :], xinb[s], writes=[xinb[s]])

        def N(t):
            p = t % 2
            k.op(k.dve, lambda e: e.memset(ss[p][:], 0.0), writes=[ssb[p]])
            for s in range(4):
                emit_norm_stats(k, C, xin[s][:], xinb[s], ss[p], ssb[p], s, hbf[s], hbfb[s])
            k.op(k.act, lambda e: e.activation(out=rs[p][:], in_=ss[p][:], func=AF.Sqrt, bias=C.eps_t[:, 0:1], scale=1.0 / D),
                 reads=[ssb[p], C.constb], writes=[rsb[p]])
            k.op(k.dve, lambda e: e.reciprocal(out=rs[p][:], in_=rs[p][:]), reads=[rsb[p]], writes=[rsb[p]])
            for s in range(4):
                k.op(k.dve, lambda e, s=s: e.scalar_tensor_tensor(out=xin[s][:], in0=xin[s][:], scalar=rs[p][:, s:s + 1], in1=a_t[:],
                                                                 op0=ALU.mult, op1=ALU.mult),
                     reads=[xinb[s], rsb[p], a_b], writes=[xinb[s]])
                k.op(k.dve, lambda e, s=s: e.tensor_tensor(out=hbf[s][:], in0=xin[s][:], in1=sh_t[:], op=ALU.add),
                     reads=[xinb[s], sh_b], writes=[hbfb[s]])

        def T(t):
            for s in range(4):
                emit_transposes(k, C, hbf[s], hbfb[s], hT, hTb[s], s, 6 + (s % 2))

        def U(t, mid=None):
            for j in range(FC):
                pb = j % 2
                for half, bank in ((0, pb), (1, 2 + pb)):
                    col = half * DFF + j * 128

                    def mm(e, col=col, bank=bank):
                        for kc in range(KC):
                            ins = e.matmul(C.ps[bank][:, :], lhsT=w13[:, kc, col:col + 128], rhs=hT[:, kc, :],
                                           start=(kc == 0), stop=(kc == KC - 1))
                        return ins
                    k.op(k.pe, mm, reads=[w13g[gof[j]]] + hTb, writes=[C.psb[bank]])
                k.op(k.act, lambda e, pb=pb: e.activation(out=sg[pb][:], in_=C.ps[pb][:, :], func=AF.Silu),
                     reads=[C.psb[pb]], writes=[sgb[pb]])
                k.op(k.dve, lambda e, pb=pb, j=j: e.tensor_tensor(out=actT[:, j, :], in0=sg[pb][:], in1=C.ps[2 + pb][:, :], op=ALU.mult),
                     reads=[sgb[pb], C.psb[2 + pb]], writes=[actb[j]])
                if mid is not None and j == 5:
                    mid()

        def Dn(t):
            for s in range(4):
                r0 = t * 512 + s * 128
                xs = s % 2
                k.dma(k.sp, xres[xs][:], src[r0:r0 + 128, :], xresb[xs], writes=[xresb[xs]])
                for n in range(2):
                    bank = 4 + (2 * s + n) % 2

                    def mm(e, s=s, n=n, bank=bank):
                        for j in range(FC):
                            ins = e.matmul(C.ps[bank][:, :], lhsT=actT[:, j, s * 128:(s + 1) * 128], rhs=w2[:, j, n * 512:(n + 1) * 512],
                                           start=(j == 0), stop=(j == FC - 1))
                        return ins
                    k.op(k.pe, mm, reads=actb + w2b, writes=[C.psb[bank]])
                    k.op(k.dve, lambda e, xs=xs, n=n, bank=bank: e.tensor_tensor(out=xres[xs][:, n * 512:(n + 1) * 512],
                                                                                  in0=xres[xs][:, n * 512:(n + 1) * 512],
                                                                                  in1=C.ps[bank][:, :], op=ALU.add),
                         reads=[xresb[xs], C.psb[bank]], writes=[xresb[xs]])
                k.dma(k.sp, dst[r0:r0 + 128, :], xres[xs][:], xresb[xs], reads=[xresb[xs]])

        L(0)
        N(0)
        T(0)
        for t in range(NT):
            nxt = None
            if t + 1 < NT:
                L(t + 1)
                nxt = (lambda t=t: N(t + 1))
            U(t, nxt)
            if t + 1 < NT:
                T(t + 1)
            Dn(t)
        k.barrier()
    k.release_dsems()


def build(S, plan, n_layers=DEPTH):
    nc = bass.Bass("TRN2", target_bir_lowering=False)
    C = Ctx()
    C.S = S
    inp = lambda name, shape, dt=F32: nc.dram_tensor(name, shape, dt, kind="ExternalInput").ap()
    C.x = inp("x", [S, D])
    C.c = inp("c", [D])
    C.ada_w = inp("ada_w", [DEPTH, D, 9 * D])
    C.ada_b = inp("ada_b", [DEPTH, 9 * D])
    C.norm_g = inp("norm_g", [DEPTH, 3, D])
    C.ffn_w13 = inp("ffn_w13", [DEPTH, 2, D, 2 * DFF])
    C.ffn_w2 = inp("ffn_w2", [DEPTH, 2, DFF, D])
    C.swa_w_in = inp("swa_w_in", [2, D, 1536])
    C.swa_w_out = inp("swa_w_out", [2, D, D])
    C.swa_q_g = inp("swa_q_g", [2, HD])
    C.swa_k_g = inp("swa_k_g", [2, HD])
    C.swa_sink = inp("swa_sink", [2, NH])
    C.swa_bias = inp("swa_bias", [128, 4096])
    C.fox_w_in = inp("fox_w_in", [2, D, 3088])
    C.fox_w_out = inp("fox_w_out", [2, D, D])
    C.fox_b_f = inp("fox_b_f", [2, NH])
    C.fox_q_g = inp("fox_q_g", [2, HD])
    C.fox_k_g = inp("fox_k_g", [2, HD])
    C.k_ident = inp("k_ident", [128, 128])
    C.k_swamask = inp("k_swamask", [128, 4096])
    C.k_negtri = inp("k_negtri", [128, 128])
    C.out = nc.dram_tensor("out", [S, D], F32, kind="ExternalOutput").ap()
    C.modv = nc.dram_tensor("modv", [DEPTH, 9 * D], F32, kind="Internal").ap()
    C.qx = nc.dram_tensor("qx", [NH, 65, S], BF16, kind="Internal").ap()
    C.kx = nc.dram_tensor("kx", [NH, 65, S], BF16, kind="Internal").ap()
    C.vx = nc.dram_tensor("vx", [NH, S, HD], BF16, kind="Internal").ap()
    C.oTd = nc.dram_tensor("oTd", [D, S], BF16, kind="Internal").ap()

    with ExitStack() as es:
        k = K(nc, es)
        C.ps = [es.enter_context(nc.psum_tensor(f"ps{i}", [128, 512], F32)) for i in range(8)]
        C.psb = [k.buf(f"ps{i}") for i in range(8)]
        C.ident = sb(nc, es, "ident", [128, 128], BF16)
        C.identf = sb(nc, es, "identf", [128, 128], F32)
        C.eps_t = sb(nc, es, "eps_t", [128, 1], F32)
        C.one_t = sb(nc, es, "one_t", [128, 1], F32)
        C.constb = k.buf("const", dma=True)
        block = es.enter_context(nc.Block())
        k.dma(k.sp, C.identf[:], C.k_ident, C.constb, writes=[C.constb])
        k.op(k.dve, lambda e: e.tensor_copy(out=C.ident[:], in_=C.identf[:]), reads=[C.constb], writes=[C.constb])
        k.op(k.dve, lambda e: e.memset(C.eps_t[:], EPS), writes=[C.constb])
        k.op(k.dve, lambda e: e.memset(C.one_t[:], 1.0), writes=[C.constb])
        keep = k.dnext
        k_release = k.release_dsems

        def release():
            k.dnext = keep
        k.release_dsems = release

        layers = sorted(set(p[1] for p in plan))
        phase_mods(k, C, layers)
        cur = C.x
        for (kind, l, i) in plan:
            if kind == "ffn":
                phase_ffn(k, C, l, i, cur, C.out)
            elif kind == "swa":
                phase_swa(k, C, l, cur, C.out)
            elif kind == "fox":
                phase_fox(k, C, l, cur, C.out)
            cur = C.out
        k.barrier()
    return nc


def emit_rstd_lnexp(k, C, ss, ssb, rs, rsb, n):
    k.op(k.act, lambda e: e.activation(out=rs[:], in_=ss[:], func=AF.Ln, bias=C.eps_t[:, 0:1], scale=1.0 / n),
         reads=[ssb, C.constb], writes=[rsb])
    k.op(k.act, lambda e: e.activation(out=rs[:], in_=rs[:], func=AF.Exp, scale=-0.5), reads=[rsb], writes=[rsb])


def emit_x_norm(k, C, xt, xb, p, ss, ssb, rs, rsb, hbf, hbfb, tmp, tmpb, a_t, a_b, sh_t, sh_b, hT, hTb, use_sqrt=False):
    k.op(k.dve, lambda e: e.memset(ss[p][:], 0.0), writes=[ssb[p]])
    for s in range(4):
        emit_norm_stats(k, C, xt[s][:], xb[s], ss[p], ssb[p], s, hbf[s % 2], hbfb[s % 2])
    emit_rstd_lnexp(k, C, ss[p], ssb[p], rs[p], rsb[p], D)
    for s in range(4):
        r = s % 2
        k.op(k.dve, lambda e, s=s, r=r: e.scalar_tensor_tensor(out=tmp[r][:], in0=xt[s][:], scalar=rs[p][:, s:s + 1], in1=a_t[:],
                                                                 op0=ALU.mult, op1=ALU.mult),
             reads=[xb[s], rsb[p], a_b], writes=[tmpb[r]])
        k.op(k.dve, lambda e, r=r: e.tensor_tensor(out=hbf[r][:], in0=tmp[r][:], in1=sh_t[:], op=ALU.add),
             reads=[tmpb[r], sh_b], writes=[hbfb[r]])
        emit_transposes(k, C, hbf[r], hbfb[r], hT, hTb[s], s, None)


def emit_qk_chunk(k, C, n, wt, wtb, col, hT, hTb, bones, sq, sqb, lnt, lntb, gcol, gb, dsts, dst_buf, pend, post=None):
    bank = C.nb()
    r = n % len(sq)

    def mm(e):
        for kc in range(KC):
            ins = e.matmul(C.ps[bank][:, :], lhsT=wt[:, kc, col:col + 128], rhs=hT[:, kc, :], start=(kc == 0), stop=(kc == KC - 1))
        return ins
    k.op(k.pe, mm, reads=[wtb] + hTb, writes=[C.psb[bank]])
    k.op(k.act, lambda e: e.activation(out=sq[r][:], in_=C.ps[bank][:, :], func=AF.Square), reads=[C.psb[bank]], writes=[sqb[r]])
    while len(pend) >= QK_DEPTH:
        pend.pop(0)()

    def tail():
        ssbank = C.nb()
        k.op(k.pe, lambda e: e.matmul(C.ps[ssbank][:, :], lhsT=bones[:], rhs=sq[r][:], start=True, stop=True),
             reads=[sqb[r], C.constb], writes=[C.psb[ssbank]])
        k.op(k.act, lambda e: e.activation(out=lnt[r][:], in_=C.ps[ssbank][:, :], func=AF.Ln, bias=C.eps_t[:, 0:1], scale=1.0 / HD),
             reads=[C.psb[ssbank], C.constb], writes=[lntb[r]])
        k.op(k.act, lambda e: e.activation(out=lnt[r][:], in_=lnt[r][:], func=AF.Exp, scale=-0.5), reads=[lntb[r]], writes=[lntb[r]])
        for (dst_ap, sl) in dsts:
            k.op(k.dve, lambda e, dst_ap=dst_ap, sl=sl: e.scalar_tensor_tensor(out=dst_ap, in0=C.ps[bank][sl, :], scalar=gcol[sl, 0:1],
                                                                               in1=lnt[r][sl, :], op0=ALU.mult, op1=ALU.mult),
                 reads=[C.psb[bank], lntb[r], gb], writes=[dst_buf])
        C.fb(ssbank)
        C.fb(bank)
        if post is not None:
            post()
    pend.append(tail)


def load_gcol(k, C, es, name, src_1d, scale):
    nc = k.nc
    t = sb(nc, es, name, [128, 1], F32)
    b = k.buf(name, dma=True)
    v = src_1d.rearrange("(p o) -> p o", o=1)
    k.dma(k.sp, t[0:64, :], v, b, writes=[b])
    k.dma(k.sp, t[64:128, :], v, b, writes=[b])
    if scale != 1.0:
        k.op(k.dve, lambda e: e.tensor_scalar(out=t[:], in0=t[:], scalar1=float(scale), scalar2=None, op0=ALU.mult), reads=[b], writes=[b])
    return t, b


def make_bones(k, C, es):
    nc = k.nc
    bones = sb(nc, es, "bones", [128, 128], BF16)
    k.op(k.dve, lambda e: e.memset(bones[:], 0.0), writes=[C.constb])
    k.op(k.dve, lambda e: e.memset(bones[0:64, 0:64], 1.0), writes=[C.constb])
    k.op(k.dve, lambda e: e.memset(bones[64:128, 64:128], 1.0), writes=[C.constb])
    return bones


def phase_swa(k, C, l, src, dst):
    nc = k.nc
    S = C.S
    NT = S // 512
    jl = l // 2
    with ExitStack() as es:
        xs = [[sb(nc, es, f"xs{p}_{s}", [128, D], F32) for s in range(4)] for p in range(2)]
        xsb = [[k.buf(f"xs{p}_{s}", dma=True) for s in range(4)] for p in range(2)]
        (a_t, a_b), (sh_t, sh_b), (ga_t, ga_b) = mod_prologue(k, C, es, l, 1, 1.0, ga=(xs[1][0], xsb[1][0]))
        w_in = C.swa_w_in
        wq = sb(nc, es, "wq", [128, KC, 1024], BF16)
        wqb = k.buf("wq", dma=True)
        for kc in range(KC):
            k.dma(k.pool, wq[:, kc, :], w_in[jl, kc * 128:(kc + 1) * 128, 0:1024], wqb, writes=[wqb])
        wkd = sb(nc, es, "wkd", [128, KC, 512], BF16)
        wkdb = k.buf("wkd", dma=True)
        for j in range(4):
            for hf in range(2):
                k.dma(k.pool, wkd[:, :, j * 128 + hf * 64: j * 128 + hf * 64 + 64],
                      w_in[jl, :, 1024 + j * 64:1024 + (j + 1) * 64].rearrange("(kc p) n -> p kc n", p=128), wkdb, writes=[wkdb])
        wv = sb(nc, es, "wv", [128, KC, 256], BF16)
        wvb = k.buf("wv", dma=True)
        k.dma(k.pool, wv[:], w_in[jl, :, 1280:1536].rearrange("(kc p) n -> p kc n", p=128), wvb, writes=[wvb])
        wo = sb(nc, es, "wo", [128, 8, D], BF16)
        wob = k.buf("wo")
        stage = [xs[1][1], xs[1][2]]
        stage_b = [xsb[1][1], xsb[1][2]]
        cnt = 0
        for jp in range(2):
            for g in range(4):
                for hf in range(2):
                    head = 4 * (2 * jp + hf) + g
                    load_scaled_weight(k, C, stage, stage_b, cnt, wo[hf * 64:(hf + 1) * 64, jp * 4 + g, :], wob,
                                       C.swa_w_out[jl, head * 64:(head + 1) * 64, :], ga_t, ga_b, rows=64, p0=hf * 64)
                    cnt += 1
        gq, gqb = load_gcol(k, C, es, "gq", C.swa_q_g[jl, :], HD ** -0.5)
        gk, gkb = load_gcol(k, C, es, "gk", C.swa_k_g[jl, :], 1.0)
        bones = make_bones(k, C, es)
        tmp = [sb(nc, es, f"tmp{j}", [128, D], F32) for j in range(2)]
        tmpb = [k.buf(f"tmp{j}", dma=True) for j in range(2)]
        tmp_pro, tmp_prob = tmp, tmpb
        BM = sb(nc, es, "BM", [128, 4096], BF16)
        BMb = k.buf("BM")
        for s in range(4):
            pass
        for s in range(4):
            bt_, bb_ = xs[0][s], xsb[0][s]
            mt_, mb_ = tmp_pro[s % 2], tmp_prob[s % 2]
            k.dma(k.sp, bt_[:], C.swa_bias[:, s * 1024:(s + 1) * 1024], bb_, writes=[bb_])
            k.dma(k.sp, mt_[:], C.k_swamask[:, s * 1024:(s + 1) * 1024], mb_, writes=[mb_])
            k.op(k.dve, lambda e, bt_=bt_, mt_=mt_: e.tensor_tensor(out=bt_[:], in0=bt_[:], in1=mt_[:], op=ALU.mult),
                 reads=[bb_, mb_], writes=[bb_])
            k.op(k.dve, lambda e, mt_=mt_: e.tensor_scalar(out=mt_[:], in0=mt_[:], scalar1=-NEG, scalar2=NEG, op0=ALU.mult, op1=ALU.add),
                 reads=[mb_], writes=[mb_])
            k.op(k.dve, lambda e, s=s, bt_=bt_, mt_=mt_: e.tensor_tensor(out=BM[:, s * 1024:(s + 1) * 1024], in0=bt_[:], in1=mt_[:], op=ALU.add),
                 reads=[bb_, mb_], writes=[BMb])
        s16 = sb(nc, es, "s16", [1, NH], F32)
        ESr = sb(nc, es, "ESr", [1, NH, 128], BF16)
        sel = sb(nc, es, "sel", [1, 2, 128], BF16)
        ESb = k.buf("ES", dma=True)
        k.dma(k.sp, s16[:], C.swa_sink[jl:jl + 1, :], ESb, writes=[ESb])
        k.op(k.act, lambda e: e.activation(out=s16[:], in_=s16[:], func=AF.Exp), reads=[ESb], writes=[ESb])
        k.op(k.dve, lambda e: e.tensor_copy(out=ESr[:], in_=s16[:].unsqueeze(2).to_broadcast([1, NH, 128])), reads=[ESb], writes=[ESb])
        k.op(k.dve, lambda e: e.memset(sel[:], 0.0), writes=[ESb])
        k.op(k.dve, lambda e: e.memset(sel[0:1, 0, 64:128], 1.0), writes=[ESb])
        k.op(k.dve, lambda e: e.memset(sel[0:1, 1, 0:64], 1.0), writes=[ESb])

        vext = sb(nc, es, "vext", [128, 8, 4, 128], BF16)
        vextb = [k.buf(f"vext{i}") for i in range(8)]
        k.op(k.dve, lambda e: e.memset(vext[:], 1.0), writes=vextb)
        kTa = sb(nc, es, "kTa", [128, 4, 1024], BF16)
        kTb = sb(nc, es, "kTb", [128, 4, 1024], BF16)
        k.op(k.dve, lambda e: e.memset(kTa[:], 0.0), writes=[C.constb])
        k.op(k.dve, lambda e: e.memset(kTb[:], 0.0), writes=[C.constb])
        kTdb = [[k.buf(f"kTd{j}_{h}") for h in range(2)] for j in range(4)]
        qT2 = [sb(nc, es, f"qT{p}", [128, 8, 512], BF16) for p in range(2)]
        qTb2 = [[k.buf(f"qT{p}_{c}") for c in range(8)] for p in range(2)]
        hbf = [sb(nc, es, f"hbf{j}", [128, D], BF16) for j in range(2)]
        hbfb = [k.buf(f"hbf{j}") for j in range(2)]
        hT = sb(nc, es, "hT", [128, KC, 512], BF16)
        hTb = [k.buf(f"hT{j}") for j in range(4)]
        sq = [sb(nc, es, f"sq{j}", [128, 512], BF16) for j in range(3)]
        sqb = [k.buf(f"sq{j}") for j in range(3)]
        lnt = [sb(nc, es, f"lnt{j}", [128, 512], F32) for j in range(3)]
        lntb = [k.buf(f"lnt{j}") for j in range(3)]
        NPT = 6
        pt = [sb(nc, es, f"pt{j}", [128, 512], BF16) for j in range(NPT)]
        ptb = [k.buf(f"pt{j}") for j in range(NPT)]
        ptc = [0]
        rd = [sb(nc, es, f"rd{j}", [128, 512], F32) for j in range(2)]
        rdb = [k.buf(f"rd{j}") for j in range(2)]
        oT = [sb(nc, es, f"oT{j}", [128, 8, 128], BF16) for j in range(2)]
        oTb = [k.buf(f"oT{j}") for j in range(2)]
        ss = [sb(nc, es, f"ss{j}", [128, 4], F32) for j in range(2)]
        ssb = [k.buf(f"ss{j}") for j in range(2)]
        rs = [sb(nc, es, f"rs{j}", [128, 4], F32) for j in range(2)]
        rsb = [k.buf(f"rs{j}") for j in range(2)]

        def L(t):
            for s in range(4):
                r0 = t * 512 + s * 128
                k.dma(k.sp, xs[t % 2][s][:], src[r0:r0 + 128, :], xsb[t % 2][s], writes=[xsb[t % 2][s]])

        def sT_unit(t, s, j, u):
            b = 4 * t + s
            qT, qTb = qT2[t % 2], qTb2[t % 2]
            kbs0 = [(1, b % 8)] if b == 0 else [(0, (b - 1) % 8), (1, b % 8)]
            kbs = []
            for (kbi, kslot) in kbs0:
                bank = C.nb()
                r = ptc[0] % NPT
                ptc[0] += 1
                kbs.append((kbi, kslot, bank, r))

                def mm(e, kslot=kslot, bank=bank, kbi=kbi):
                    off = (j * 2 + kbi) * 512
                    e.matmul(C.ps[bank][:, :], lhsT=C.ident[:], rhs=BM[:, off:off + 512], start=True, stop=False)
                    for g in range(4):
                        head = 4 * j + g
                        c, hf = head // 2, head % 2
                        kT_ = kTa if hf == 0 else kTb
                        ins = e.matmul(C.ps[bank][:, g * 128:(g + 1) * 128],
                                       lhsT=kT_[:, j, kslot * 128:(kslot + 1) * 128],
                                       rhs=qT[:, c, s * 128:(s + 1) * 128], start=False, stop=(g == 3))
                    return ins
                k.op(k.pe, mm, reads=[kTdb[j][kslot // 4], qTb[2 * j], qTb[2 * j + 1], C.constb, BMb], writes=[C.psb[bank]])
            return kbs

        def rest_unit(t, s, j, u, kbs):
            b = 4 * t + s
            jp = j // 2
            accb = C.nb()
            num = slice(0, 64) if j % 2 == 0 else slice(64, 128)
            den = slice(64, 128) if j % 2 == 0 else slice(0, 64)
            for (kbi, kslot, bank, r) in kbs:
                k.op(k.act, lambda e, r=r, bank=bank: e.activation(out=pt[r][:], in_=C.ps[bank][:, :], func=AF.Exp),
                     reads=[C.psb[bank]], writes=[ptb[r]])
                C.fb(bank)

            def pv(e):
                for idx, (kbi, kslot, bank, r) in enumerate(kbs):
                    e.matmul(C.ps[accb][:, :], lhsT=vext[:, kslot, j, :], rhs=pt[r][:], start=(idx == 0), stop=False)
                return e.matmul(C.ps[accb][:, :], lhsT=sel[0:1, j % 2, :], rhs=ESr[0:1, j * 4:(j + 1) * 4, :].rearrange("p g q -> p (g q)"),
                                start=False, stop=True)
            k.op(k.pe, pv, reads=[ptb[r] for (_, _, _, r) in kbs] + [vextb[ks] for (_, ks, _, _) in kbs] + [ESb], writes=[C.psb[accb]])
            r2 = u % 2
            acc3 = C.ps[accb][:, :].rearrange("p (g q) -> p g q", g=4)
            rd3 = rd[r2][:].rearrange("p (g q) -> p g q", g=4)
            k.op(k.dve, lambda e: e.reciprocal(out=rd3[den], in_=acc3[den]), reads=[C.psb[accb]], writes=[rdb[r2]])
            k.op(k.dve, lambda e: e.tensor_tensor(out=oT[b % 2][num, jp * 4:(jp + 1) * 4, :], in0=acc3[num], in1=rd3[den], op=ALU.mult),
                 reads=[C.psb[accb], rdb[r2]], writes=[oTb[b % 2]])
            C.fb(accb)

        def outproj(t, s):
            b = 4 * t + s
            r0 = t * 512 + s * 128
            xt, xb = xs[t % 2][s], xsb[t % 2][s]
            for n in range(2):
                bank = C.nb()

                def mm(e, n=n, bank=bank):
                    for ci in range(8):
                        ins = e.matmul(C.ps[bank][:, :], lhsT=oT[b % 2][:, ci, :], rhs=wo[:, ci, n * 512:(n + 1) * 512],
                                       start=(ci == 0), stop=(ci == 7))
                    return ins
                k.op(k.pe, mm, reads=[oTb[b % 2], wob], writes=[C.psb[bank]])
                k.op(k.dve, lambda e, n=n, bank=bank: e.tensor_tensor(out=xt[:, n * 512:(n + 1) * 512], in0=xt[:, n * 512:(n + 1) * 512],
                                                                      in1=C.ps[bank][:, :], op=ALU.add),
                     reads=[xb, C.psb[bank]], writes=[xb])
                C.fb(bank)
            k.dma(k.sp, dst[r0:r0 + 128, :], xt[:], xb, reads=[xb])

        def front(t):
            p = t % 2
            xt, xb = xs[p], xsb[p]
            qT, qTb = qT2[p], qTb2[p]
            k.op(k.dve, lambda e: e.memset(ss[p][:], 0.0), writes=[ssb[p]])
            for s in range(4):
                emit_norm_stats(k, C, xt[s][:], xb[s], ss[p], ssb[p], s, hbf[s % 2], hbfb[s % 2])
            emit_rstd_lnexp(k, C, ss[p], ssb[p], rs[p], rsb[p], D)
            yield
            for s in range(4):
                r = s % 2
                k.op(k.dve, lambda e, s=s, r=r: e.scalar_tensor_tensor(out=tmp[r][:], in0=xt[s][:], scalar=rs[p][:, s:s + 1], in1=a_t[:],
                                                                         op0=ALU.mult, op1=ALU.mult),
                     reads=[xb[s], rsb[p], a_b], writes=[tmpb[r]])
                k.op(k.dve, lambda e, r=r: e.tensor_tensor(out=hbf[r][:], in0=tmp[r][:], in1=sh_t[:], op=ALU.add),
                     reads=[tmpb[r], sh_b], writes=[hbfb[r]])
                emit_transposes(k, C, hbf[r], hbfb[r], hT, hTb[s], s, None)
                yield
            pend = []
            n = 0
            for c in range(8):
                emit_qk_chunk(k, C, n, wq, wqb, c * 128, hT, hTb, bones, sq, sqb, lnt, lntb, gq, gqb, [(qT[:, c, :], slice(0, 128))], qTb[c], pend)
                n += 1
                yield
            for j in range(4):
                c0 = p * 512
                emit_qk_chunk(k, C, n, wkd, wkdb, j * 128, hT, hTb, bones, sq, sqb, lnt, lntb, gk, gkb,
                              [(kTa[0:64, j, c0:c0 + 512], slice(0, 64)), (kTb[64:128, j, c0:c0 + 512], slice(64, 128))],
                              kTdb[j][p], pend)
                n += 1
                yield
            for s in range(4):
                bank = C.nb()
                rb = (4 * t + s) % 8

                def mm(e, s=s, bank=bank):
                    for kc in range(KC):
                        ins = e.matmul(C.ps[bank][:, 0:256], lhsT=hT[:, kc, s * 128:(s + 1) * 128], rhs=wv[:, kc, :],
                                       start=(kc == 0), stop=(kc == KC - 1))
                    return ins
                k.op(k.pe, mm, reads=[wvb, hTb[s]], writes=[C.psb[bank]])
                if pend:
                    pend.pop(0)()
                for j in range(4):
                    c0 = 0 if j % 2 == 0 else 64
                    k.op(k.act, lambda e, j=j, c0=c0, rb=rb, bank=bank: e.copy(out=vext[:, rb, j, c0:c0 + 64], in_=C.ps[bank][:, j * 64:(j + 1) * 64]),
                         reads=[C.psb[bank]], writes=[vextb[rb]])
                C.fb(bank)
                yield
            while pend:
                pend.pop(0)()

        def drain(g):
            for _ in g:
                pass

        L(0)
        drain(front(0))
        ucount = 0
        for t in range(NT):
            nxt = None
            if t + 1 < NT:
                L(t + 1)
                nxt = front(t + 1)
            units = [(s, j) for s in range(4) for j in range(4)]
            kb_next = sT_unit(t, units[0][0], units[0][1], ucount)
            nsteps = [int(os.environ.get("SWA_NSTEPS", "99"))]

            def step():
                if nxt is not None and nsteps[0] > 0:
                    nsteps[0] -= 1
                    next(nxt, None)
            for ui, (s, j) in enumerate(units):
                u = ucount
                kbs = kb_next
                if ui + 1 < len(units):
                    kb_next = sT_unit(t, units[ui + 1][0], units[ui + 1][1], u + 1)
                rest_unit(t, s, j, u, kbs)
                ucount += 1
                step()
                if j == 3:
                    outproj(t, s)
                    step()
            if nxt is not None:
                drain(nxt)
        k.barrier()
    k.release_dsems()


def phase_fox(k, C, l, src, dst):
    nc = k.nc
    S = C.S
    NB = S // 128
    with ExitStack() as esl:
        negF = sb(nc, esl, "negF", [128, NB, NH], F32)
        negFb = k.buf("negF")
        fox_A(k, C, l, src, negF, negFb)
        fox_B(k, C, l, negF, negFb)
        fox_C(k, C, l, src, dst)


def fox_A(k, C, l, src, negF, negFb):
    nc = k.nc
    S = C.S
    NT = S // 512
    jl = l // 2
    with ExitStack() as es:
        xs = [[sb(nc, es, f"xs{p}_{s}", [128, D], F32) for s in range(4)] for p in range(2)]
        xsb = [[k.buf(f"xs{p}_{s}", dma=True) for s in range(4)] for p in range(2)]
        (a_t, a_b), (sh_t, sh_b), _ = mod_prologue(k, C, es, l, 1, 1.0, ga=(xs[1][0], xsb[1][0]))
        w_in = C.fox_w_in
        wts = []
        for nm, c0 in (("wq", 0), ("wk", 1024), ("wv", 2048)):
            wt = sb(nc, es, nm, [128, KC, 1024], BF16)
            wb = k.buf(nm, dma=True)
            for kc in range(KC):
                k.dma(k.pool, wt[:, kc, :], w_in[jl, kc * 128:(kc + 1) * 128, c0:c0 + 1024], wb, writes=[wb])
            wts.append((wt, wb))
        (wq, wqb), (wk, wkb), (wv, wvb) = wts
        wf = sb(nc, es, "wf", [128, KC, NH], BF16)
        wfb = k.buf("wf", dma=True)
        k.dma(k.pool, wf[:], w_in[jl, :, 3072:3088].rearrange("(kc p) n -> p kc n", p=128), wfb, writes=[wfb])
        gq, gqb = load_gcol(k, C, es, "gq", C.fox_q_g[jl, :], HD ** -0.5)
        gk, gkb = load_gcol(k, C, es, "gk", C.fox_k_g[jl, :], 1.0)
        bones = make_bones(k, C, es)
        negbf = sb(nc, es, "negbf", [NH, 1], F32)
        nbb = k.buf("negbf", dma=True)
        k.dma(k.sp, negbf[:], C.fox_b_f[jl, :].rearrange("(p o) -> p o", o=1), nbb, writes=[nbb])
        k.op(k.dve, lambda e: e.tensor_scalar(out=negbf[:], in0=negbf[:], scalar1=-1.0, scalar2=None, op0=ALU.mult), reads=[nbb], writes=[nbb])
        ones16 = sb(nc, es, "ones16", [NH, 512], F32)
        onesbf = sb(nc, es, "onesbf", [NH, 512], BF16)
        negI = sb(nc, es, "negI", [NH, NH], F32)
        cb2 = k.buf("fconst", dma=True)
        k.op(k.dve, lambda e: e.memset(ones16[:], 1.0), writes=[cb2])
        k.op(k.dve, lambda e: e.memset(onesbf[:], 1.0), writes=[cb2])
        k.op(k.dve, lambda e: e.tensor_scalar(out=negI[:], in0=C.identf[0:NH, 0:NH], scalar1=-1.0, scalar2=None, op0=ALU.mult),
             reads=[C.constb], writes=[cb2])

        hbf = [sb(nc, es, f"hbf{j}", [128, D], BF16) for j in range(2)]
        hbfb = [k.buf(f"hbf{j}") for j in range(2)]
        hT = sb(nc, es, "hT", [128, KC, 512], BF16)
        hTb = [k.buf(f"hT{j}") for j in range(4)]
        tmp = [sb(nc, es, f"tmp{j}", [128, D], F32) for j in range(2)]
        tmpb = [k.buf(f"tmp{j}") for j in range(2)]
        sq = [sb(nc, es, f"sq{j}", [128, 512], BF16) for j in range(3)]
        sqb = [k.buf(f"sq{j}") for j in range(3)]
        lnt = [sb(nc, es, f"lnt{j}", [128, 512], F32) for j in range(3)]
        lntb = [k.buf(f"lnt{j}") for j in range(3)]
        NQ = 4
        qn = [sb(nc, es, f"qn{j}", [128, 512], BF16) for j in range(NQ)]
        qnb = [k.buf(f"qn{j}", dma=True) for j in range(NQ)]
        vt = [sb(nc, es, f"vt{j}", [128, 512], BF16) for j in range(2)]
        vtb = [k.buf(f"vt{j}", dma=True) for j in range(2)]
        lf = [sb(nc, es, f"lf{j}", [NH, 512], F32) for j in range(2)]
        lfb = [k.buf(f"lf{j}") for j in range(2)]
        FT = [sb(nc, es, f"FT{j}", [NH, 512], F32) for j in range(2)]
        FTb = [k.buf(f"FT{j}") for j in range(2)]
        Rb = [sb(nc, es, f"Rb{j}", [NH, 512], BF16) for j in range(2)]
        Rbb = [k.buf(f"Rb{j}", dma=True) for j in range(2)]
        ss = [sb(nc, es, f"ss{j}", [128, 4], F32) for j in range(2)]
        ssb = [k.buf(f"ss{j}") for j in range(2)]
        rs = [sb(nc, es, f"rs{j}", [128, 4], F32) for j in range(2)]
        rsb = [k.buf(f"rs{j}") for j in range(2)]

        def L(t):
            for s in range(4):
                r0 = t * 512 + s * 128
                k.dma(k.sp, xs[t % 2][s][:], src[r0:r0 + 128, :], xsb[t % 2][s], writes=[xsb[t % 2][s]])

        L(0)
        n = 0
        qi = 0
        for t in range(NT):
            if t + 1 < NT:
                L(t + 1)
            c0 = t * 512
            emit_x_norm(k, C, xs[t % 2], xsb[t % 2], t % 2, ss, ssb, rs, rsb, hbf, hbfb, tmp, tmpb, a_t, a_b, sh_t, sh_b, hT, hTb)
            pend = []
            for (wt, wb, gcol, gb, dram) in ((wq, wqb, gq, gqb, C.qx), (wk, wkb, gk, gkb, C.kx)):
                for c in range(8):
                    r = qi % NQ
                    qi += 1

                    def post(r=r, c=c, dram=dram):
                        for hf in range(2):
                            k.dma(k.sp, dram[2 * c + hf, 0:64, c0:c0 + 512], qn[r][hf * 64:(hf + 1) * 64, :], qnb[r], reads=[qnb[r]])
                    emit_qk_chunk(k, C, n, wt, wb, c * 128, hT, hTb, bones, sq, sqb, lnt, lntb, gcol, gb,
                                  [(qn[r][:], slice(0, 128))], qnb[r], pend, post=post)
                    n += 1
            fb = C.nb()

            def mmf(e):
                for kc in range(KC):
                    ins = e.matmul(C.ps[fb][0:NH, :], lhsT=wf[:, kc, :], rhs=hT[:, kc, :], start=(kc == 0), stop=(kc == KC - 1))
                return ins
            k.op(k.pe, mmf, reads=[wfb] + hTb, writes=[C.psb[fb]])
            pend.pop(0)()
            p = t % 2
            k.op(k.act, lambda e, p=p: e.activation(out=lf[p][:], in_=C.ps[fb][0:NH, :], func=AF.Exp, bias=negbf[:, 0:1], scale=-1.0),
                 reads=[C.psb[fb], nbb], writes=[lfb[p]])
            k.op(k.act, lambda e, p=p: e.activation(out=lf[p][:], in_=lf[p][:], func=AF.Ln, bias=C.one_t[0:NH, 0:1], scale=1.0),
                 reads=[lfb[p], C.constb], writes=[lfb[p]])
            init = 0.0 if t == 0 else FT[1 - p][:, 511:512]
            k.op(k.dve, lambda e, p=p, init=init: e.tensor_tensor_scan(out=FT[p][:], data0=ones16[:], data1=lf[p][:], initial=init,
                                                                      op0=ALU.mult, op1=ALU.subtract),
                 reads=[lfb[p], cb2, FTb[1 - p]], writes=[FTb[p]])
            k.op(k.dve, lambda e, p=p: e.tensor_copy(out=Rb[p][:], in_=FT[p][:]), reads=[FTb[p]], writes=[Rbb[p]])
            k.dma(k.sp, C.qx[:, 64, c0:c0 + 512], Rb[p][:], Rbb[p], reads=[Rbb[p]])
            k.dma(k.sp, C.kx[:, 64, c0:c0 + 512], onesbf[:], cb2, reads=[cb2])
            for s in range(4):
                def mmt(e, s=s, p=p):
                    return e.matmul(C.ps[fb][:, 16 + s * 16:32 + s * 16], lhsT=FT[p][:, s * 128:(s + 1) * 128], rhs=negI[:, :], start=True, stop=True)
                k.op(k.pe, mmt, reads=[FTb[p], cb2], writes=[C.psb[fb]])
            k.op(k.act, lambda e: e.copy(out=negF[:, 4 * t:4 * t + 4, :], in_=C.ps[fb][:, 16:80].rearrange("p (s h) -> p s h", s=4)),
                 reads=[C.psb[fb]], writes=[negFb])
            C.fb(fb)
            for s in range(4):
                r0 = t * 512 + s * 128
                for n2 in range(2):
                    bank = C.nb()
                    n += 1
                    vr = (2 * s + n2) % 2

                    def mm(e, s=s, n2=n2, bank=bank):
                        for kc in range(KC):
                            ins = e.matmul(C.ps[bank][:, :], lhsT=hT[:, kc, s * 128:(s + 1) * 128], rhs=wv[:, kc, n2 * 512:(n2 + 1) * 512],
                                           start=(kc == 0), stop=(kc == KC - 1))
                        return ins
                    k.op(k.pe, mm, reads=[wvb, hTb[s]], writes=[C.psb[bank]])
                    if pend:
                        pend.pop(0)()
                    k.op(k.act, lambda e, vr=vr, bank=bank: e.copy(out=vt[vr][:], in_=C.ps[bank][:, :]), reads=[C.psb[bank]], writes=[vtb[vr]])
                    C.fb(bank)
                    k.dma(k.sp, C.vx[n2 * 8:(n2 + 1) * 8, r0:r0 + 128, :].rearrange("h t d -> t h d"),
                          vt[vr][:].rearrange("p (h d) -> p h d", h=8), vtb[vr], reads=[vtb[vr]])
            while pend:
                pend.pop(0)()
        k.barrier()
    k.release_dsems()


def fox_B(k, C, l, negF, negFb):
    nc = k.nc
    S = C.S
    NT = S // 512
    NB = S // 128
    with ExitStack() as es:
        qs = [sb(nc, es, f"qs{j}", [65, S], BF16) for j in range(2)]
        ks = [sb(nc, es, f"ks{j}", [65, S], BF16) for j in range(2)]
        vs = [sb(nc, es, f"vs{j}", [128, NB, 128], BF16) for j in range(2)]
        qsb = [k.buf(f"qs{j}", dma=True) for j in range(2)]
        ksb = [k.buf(f"ks{j}", dma=True) for j in range(2)]
        vsb = [k.buf(f"vs{j}", dma=True) for j in range(2)]
        k.op(k.dve, lambda e: e.memset(vs[0][:, :, 64:128], 1.0), writes=[vsb[0]])
        k.op(k.dve, lambda e: e.memset(vs[1][:, :, 0:64], 1.0), writes=[vsb[1]])
        negtri = sb(nc, es, "negtri", [128, 128], F32)
        ntb = k.buf("negtri", dma=True)
        k.dma(k.sp, negtri[:], C.k_negtri, ntb, writes=[ntb])
        NP = 3
        pT = [sb(nc, es, f"pT{j}", [128, 512], BF16) for j in range(NP)]
        pTb = [k.buf(f"pT{j}") for j in range(NP)]
        rdn = [sb(nc, es, f"rdn{j}", [128, 512], F32) for j in range(2)]
        rdnb = [k.buf(f"rdn{j}") for j in range(2)]
        oTs = [sb(nc, es, f"oTs{j}", [128, 512], BF16) for j in range(2)]
        oTsb = [k.buf(f"oTs{j}", dma=True) for j in range(2)]

        def load(h):
            hb = h % 2
            k.dma(k.sp, qs[hb][:], C.qx[h, :, :], qsb[hb], writes=[qsb[hb]])
            k.dma(k.sp, ks[hb][:], C.kx[h, :, :], ksb[hb], writes=[ksb[hb]])
            vc = 0 if hb == 0 else 64
            k.dma(k.sp, vs[hb][:, :, vc:vc + 64], C.vx[h, :, :].rearrange("(b p) d -> p b d", p=128), vsb[hb], writes=[vsb[hb]])

        load(0)
        tcount = 0
        ucount = 0
        for h in range(NH):
            hb = h % 2
            if h + 1 < NH:
                load(h + 1)
            num = slice(0, 64) if hb == 0 else slice(64, 128)
            den = slice(64, 128) if hb == 0 else slice(0, 64)
            units = [(t, kb) for t in range(NT) for kb in range(4 * t + 4)]

            def qk(ui):
                t, kb = units[ui]
                u = ucount + ui
                bank = u % NP
                c0 = 128 * (kb - 4 * t) if kb >= 4 * t else 0
                k.op(k.pe, lambda e: e.matmul(C.ps[bank][:, c0:512], lhsT=ks[hb][:, kb * 128:(kb + 1) * 128],
                                              rhs=qs[hb][:, t * 512 + c0:(t + 1) * 512], start=True, stop=True),
                     reads=[ksb[hb], qsb[hb]], writes=[C.psb[bank]])

            qk(0)
            if len(units) > 1:
                qk(1)
            for ui, (t, kb) in enumerate(units):
                u = ucount + ui
                bank = u % NP
                r = u % NP
                diag = kb >= 4 * t
                c0 = 128 * (kb - 4 * t) if diag else 0
                accb = 4 + (tcount + t) % 2
                if ui + 2 < len(units):
                    qk(ui + 2)
                if diag:
                    k.op(k.dve, lambda e: e.tensor_tensor(out=C.ps[bank][:, c0:c0 + 128], in0=C.ps[bank][:, c0:c0 + 128], in1=negtri[:], op=ALU.add),
                         reads=[C.psb[bank], ntb], writes=[C.psb[bank]])
                k.op(k.act, lambda e: e.activation(out=pT[r][:, c0:512], in_=C.ps[bank][:, c0:512], func=AF.Exp, bias=negF[:, kb, h:h + 1], scale=1.0),
                     reads=[C.psb[bank], negFb], writes=[pTb[r]])
                last = (kb == 4 * t + 3)
                k.op(k.pe, lambda e: e.matmul(C.ps[accb][:, c0:512], lhsT=vs[hb][:, kb, :], rhs=pT[r][:, c0:512], start=(kb == 0), stop=last),
                     reads=[pTb[r], vsb[hb]], writes=[C.psb[accb]])
                if last:
                    r2 = (tcount + t) % 2
                    k.op(k.dve, lambda e: e.reciprocal(out=rdn[r2][den, :], in_=C.ps[accb][den, :]), reads=[C.psb[accb]], writes=[rdnb[r2]])
                    k.op(k.dve, lambda e: e.tensor_tensor(out=oTs[r2][num, :], in0=C.ps[accb][num, :], in1=rdn[r2][den, :], op=ALU.mult),
                         reads=[C.psb[accb], rdnb[r2]], writes=[oTsb[r2]])
                    k.dma(k.sp, C.oTd[h * 64:(h + 1) * 64, t * 512:(t + 1) * 512], oTs[r2][num, :], oTsb[r2], reads=[oTsb[r2]])
            ucount += len(units)
            tcount += NT
        k.barrier()
    k.release_dsems()


def fox_C(k, C, l, src, dst):
    nc = k.nc
    S = C.S
    NT = S // 512
    jl = l // 2
    with ExitStack() as es:
        xs = [[sb(nc, es, f"xs{p}_{s}", [128, D], F32) for s in range(4)] for p in range(2)]
        xsb = [[k.buf(f"xs{p}_{s}", dma=True) for s in range(4)] for p in range(2)]
        _, _, (ga_t, ga_b) = mod_prologue(k, C, es, l, 1, 1.0, ga=(xs[1][0], xsb[1][0]))
        wo = sb(nc, es, "wo", [128, 8, D], BF16)
        wob = k.buf("wo")
        stage = [xs[1][1], xs[1][2]]
        stage_b = [xsb[1][1], xsb[1][2]]
        for c in range(8):
            load_scaled_weight(k, C, stage, stage_b, c, wo[:, c, :], wob, C.fox_w_out[jl, c * 128:(c + 1) * 128, :], ga_t, ga_b)
        oTt = [sb(nc, es, f"oTt{j}", [128, 8, 512], BF16) for j in range(2)]
        oTtb = [k.buf(f"oTt{j}", dma=True) for j in range(2)]

        def L(t):
            k.dma(k.sp, oTt[t % 2][:], C.oTd[:, t * 512:(t + 1) * 512].rearrange("(c p) t -> p c t", p=128), oTtb[t % 2], writes=[oTtb[t % 2]])
            for s in range(4):
                r0 = t * 512 + s * 128
                k.dma(k.sp, xs[t % 2][s][:], src[r0:r0 + 128, :], xsb[t % 2][s], writes=[xsb[t % 2][s]])

        L(0)
        for t in range(NT):
            if t + 1 < NT:
                L(t + 1)
            for s in range(4):
                r0 = t * 512 + s * 128
                xt, xb = xs[t % 2][s], xsb[t % 2][s]
                for n in range(2):
                    bank = (2 * s + n) % 4

                    def mm(e, s=s, n=n, bank=bank):
                        for c in range(8):
                            ins = e.matmul(C.ps[bank][:, :], lhsT=oTt[t % 2][:, c, s * 128:(s + 1) * 128], rhs=wo[:, c, n * 512:(n + 1) * 512],
                                           start=(c == 0), stop=(c == 7))
                        return ins
                    k.op(k.pe, mm, reads=[oTtb[t % 2], wob], writes=[C.psb[bank]])
                    k.op(k.dve, lambda e, xt=xt, n=n, bank=bank: e.tensor_tensor(out=xt[:, n * 512:(n + 1) * 512], in0=xt[:, n * 512:(n + 1) * 512],
                                                                                in1=C.ps[bank][:, :], op=ALU.add),
                         reads=[xb, C.psb[bank]], writes=[xb])
                k.dma(k.sp, dst[r0:r0 + 128, :], xt[:], xb, reads=[xb])
        k.barrier()
    k.release_dsems()


def rel_bucket_np(dist):
    n = np.maximum(dist, 0)
    max_exact = 16
    nf = np.maximum(n, 1).astype(np.float32)
    large = max_exact + (np.log(nf / max_exact) / math.log(128 / max_exact) * (32 - max_exact)).astype(np.int32)
    large = np.minimum(large, 31)
    return np.where(n < max_exact, n, large)


def host_constants(rel_bias):
    kk = np.arange(128)[:, None]
    q = np.arange(128)[None, :]
    bias = np.zeros((128, 4, 2, 4, 128), np.float32)
    mask = np.zeros((128, 4, 2, 4, 128), np.float32)
    for kb in range(2):
        dist = q - kk + (128 if kb == 0 else 0)
        valid = (dist >= 0) & (dist < 128)
        idx = rel_bucket_np(dist)
        for j in range(4):
            for g in range(4):
                bias[:, j, kb, g, :] = rel_bias[idx, 4 * j + g]
                mask[:, j, kb, g, :] = valid
    negtri = np.where(kk <= q, 0.0, NEG).astype(np.float32)
    return dict(swa_bias=bias.reshape(128, 4096), k_swamask=mask.reshape(128, 4096),
                k_negtri=negtri, k_ident=np.eye(128, dtype=np.float32))


FULL_PLAN = []
for _l in range(DEPTH):
    FULL_PLAN += [("ffn", _l, 0), ("swa" if _l % 2 == 0 else "fox", _l, 1), ("ffn", _l, 2)]


def make_in_maps(inputs, n_cores):
    consts = host_constants(np.asarray(inputs["rel_bias"], np.float32))
    shared = {kname: np.ascontiguousarray(np.asarray(inputs[kname], np.float32)) for kname in
              ("ada_w", "ada_b", "norm_g", "ffn_w13", "ffn_w2", "swa_w_in", "swa_w_out", "swa_q_g", "swa_k_g",
               "swa_sink", "fox_w_in", "fox_w_out", "fox_b_f", "fox_q_g", "fox_k_g")}
    shared.update(consts)
    x = np.asarray(inputs["x"], np.float32)
    c = np.asarray(inputs["c"], np.float32)
    maps = []
    for b in range(n_cores):
        m = dict(shared)
        m["x"] = np.ascontiguousarray(x[b])
        m["c"] = np.ascontiguousarray(c[b])
        maps.append(m)
    return maps


def kernel(**inputs):
    x = np.asarray(inputs["x"])
    B, S, _ = x.shape
    nc = build(S, FULL_PLAN)
    maps = make_in_maps(inputs, B)
    res = run_bass_kernel_spmd(nc, maps, core_ids=list(range(B)))
    return np.stack([np.asarray(r["out"], np.float32) for r in res.results], axis=0)
```
